# Optimizing a Trainium2 kernel written in Bass

```python
import math
import jax, jax.numpy as jnp
from jax import lax
import numpy as np

D_MODEL = 2048
BATCH = 4
SEQ = 4096
DEPTH = 4

GRID_W = 64
CTX_LEN = 256
EPS = 1e-6
NA_HEADS = 16
NA_HEAD_DIM = 128
NA_WIDTH = NA_HEADS * NA_HEAD_DIM
NA_WIN_H = 8
NA_WIN_W = 16
SSD_EXPAND = 2
SSD_D_INNER = SSD_EXPAND * D_MODEL
SSD_HEAD_DIM = 64
SSD_HEADS = SSD_D_INNER // SSD_HEAD_DIM
SSD_GROUPS = 8
SSD_D_STATE = 128
SSD_CONV = 5
SSD_CHUNK = 128
SSD_BC_WIDTH = SSD_GROUPS * SSD_D_STATE
SSD_CONV_DIM = SSD_D_INNER + 2 * SSD_BC_WIDTH
D_FF = (8 * D_MODEL + 3 * 256 - 1) // (3 * 256) * 256
IN_SPLITS = (NA_WIDTH, NA_WIDTH, NA_WIDTH, SSD_D_INNER, SSD_CONV_DIM, 2 * SSD_HEADS, D_MODEL, D_MODEL)
IN_PROJ_DIM = sum(IN_SPLITS)

kernel_name = 'hybrid_na_ssd_dit_block'


def _split_cols(t, sizes):
    idx = np.cumsum(np.array(sizes))[:-1].tolist()
    return jnp.split(t, idx, axis=-1)


def _rmsnorm(x, w):
    xf = x.astype(jnp.float32)
    y = xf * lax.rsqrt(jnp.mean(xf * xf, axis=-1, keepdims=True) + EPS)
    return (y * w.astype(jnp.float32)).astype(x.dtype)


def _modulate(h, shift, scale):
    return h * (1 + scale) + shift


def _neighbourhood_attention(q, k, v, k_ctx, v_ctx, rpb):
    b, s, h, dh = q.shape
    rows = s // GRID_W
    kh = min(NA_WIN_H, rows)
    scale = dh ** -0.5
    qg = (q * scale).reshape(b, rows, GRID_W, h, dh)
    kg = k.reshape(b, rows, GRID_W, h, dh)
    vg = v.reshape(b, rows, GRID_W, h, dh)
    cols = jnp.arange(GRID_W)
    col_start = jnp.clip(cols - NA_WIN_W // 2, 0, GRID_W - NA_WIN_W)
    col_in = (cols[None, :] >= col_start[:, None]) & (cols[None, :] < col_start[:, None] + NA_WIN_W)
    dx_idx = jnp.clip(cols[None, :] - cols[:, None], -(NA_WIN_W - 1), NA_WIN_W - 1) + NA_WIN_W - 1
    k_ctx_s = k_ctx

    def row_block(r):
        r0 = jnp.clip(r - kh // 2, 0, rows - kh)
        q_r = lax.dynamic_index_in_dim(qg, r, axis=1, keepdims=False)
        k_r = lax.dynamic_slice_in_dim(kg, r0, kh, axis=1)
        v_r = lax.dynamic_slice_in_dim(vg, r0, kh, axis=1)
        dy_idx = r0 + jnp.arange(kh) - r + NA_WIN_H - 1
        bias = rpb[:, dy_idx[None, :, None], dx_idx[:, None, :]]
        s_loc = jnp.einsum('bqhd,brkhd->bhqrk', q_r, k_r).astype(jnp.float32) + bias.astype(jnp.float32)
        s_loc = jnp.where(col_in[None, None, :, None, :], s_loc, -jnp.inf)
        s_loc = s_loc.reshape(b, h, GRID_W, kh * GRID_W)
        s_ctx = jnp.einsum('bqhd,bchd->bhqc', q_r, k_ctx_s).astype(jnp.float32)
        p = jax.nn.softmax(jnp.concatenate([s_loc, s_ctx], axis=-1), axis=-1).astype(v.dtype)
        p_loc = p[..., :kh * GRID_W].reshape(b, h, GRID_W, kh, GRID_W)
        p_ctx = p[..., kh * GRID_W:]
        return (jnp.einsum('bhqrk,brkhd->bqhd', p_loc, v_r)
                + jnp.einsum('bhqc,bchd->bqhd', p_ctx, v_ctx))

    out = lax.map(row_block, jnp.arange(rows))
    return jnp.moveaxis(out, 0, 1).reshape(b, s, h, dh)


def _context_attention(q, k, v):
    scale = q.shape[-1] ** -0.5
    s = jnp.einsum('bqhd,bkhd->bhqk', q * scale, k).astype(jnp.float32)
    p = jax.nn.softmax(s, axis=-1).astype(v.dtype)
    return jnp.einsum('bhqk,bkhd->bqhd', p, v)


def _dwconv_centred(x, w, bias):
    k, ch = w.shape
    pad = k // 2
    y = lax.conv_general_dilated(x, w[:, None, :].astype(x.dtype), window_strides=(1,), padding=[(pad, pad)],
                                 dimension_numbers=('NWC', 'WIO', 'NWC'), feature_group_count=ch)
    return y + bias


def _segsum(a):
    t = a.shape[-1]
    cs = jnp.cumsum(a, axis=-1)
    seg = cs[..., :, None] - cs[..., None, :]
    return jnp.where(jnp.tril(jnp.ones((t, t), dtype=bool)), seg, -jnp.inf)


def _ssd_chunked(xs, dt, a, bm, cm, init_state):
    b, l, h, p = xs.shape
    g, n = bm.shape[2], bm.shape[3]
    e = h // g
    nc = l // SSD_CHUNK
    t = SSD_CHUNK
    xdt = (xs.astype(jnp.float32) * dt[..., None]).reshape(b, nc, t, g, e, p)
    adt = jnp.transpose((dt * a.astype(jnp.float32)).reshape(b, nc, t, g, e), (0, 3, 4, 1, 2))
    a_cs = jnp.cumsum(adt, axis=-1)
    bc = bm.astype(jnp.float32).reshape(b, nc, t, g, n)
    cc = cm.astype(jnp.float32).reshape(b, nc, t, g, n)
    decay_in = jnp.exp(_segsum(adt))
    cb = jnp.einsum('bclgn,bcsgn->bgcls', cc, bc)
    y_diag = jnp.einsum('bgcls,bgecls,bcsgep->bclgep', cb, decay_in, xdt)
    decay_states = jnp.exp(a_cs[..., -1:] - a_cs)
    states = jnp.einsum('bclgn,bgecl,bclgep->bcgepn', bc, decay_states, xdt)
    states = jnp.concatenate([init_state.astype(jnp.float32).reshape(b, 1, g, e, p, n), states], axis=1)
    chunk_tot = jnp.pad(a_cs[..., -1], ((0, 0), (0, 0), (0, 0), (1, 0)))
    decay_chunk = jnp.exp(_segsum(chunk_tot))
    new_states = jnp.einsum('bgezc,bcgepn->bzgepn', decay_chunk, states)
    states, final = new_states[:, :-1], new_states[:, -1]
    y_off = jnp.einsum('bclgn,bcgepn,bgecl->bclgep', cc, states, jnp.exp(a_cs))
    y = (y_diag + y_off).reshape(b, l, h, p)
    return y, final.reshape(b, h, p, n)


def _gated_rmsnorm(y, z, w):
    g = y * jax.nn.silu(z.astype(jnp.float32))
    b, l, d = g.shape
    g = g.reshape(b, l, SSD_GROUPS, d // SSD_GROUPS)
    g = g * lax.rsqrt(jnp.mean(g * g, axis=-1, keepdims=True) + EPS)
    return g.reshape(b, l, d) * w.astype(jnp.float32)


def _bidirectional_ssd(z, xbc, dt_raw, zc, xbcc, dtc_raw, conv_w, conv_b, dt_bias, a_log, d_skip, ssd_norm,
                       with_ctx_out):
    a = -jnp.exp(a_log.astype(jnp.float32))

    def prep(xbc_t, dt_t):
        u = jax.nn.silu(_dwconv_centred(xbc_t, conv_w, conv_b))
        xs, bm, cm = _split_cols(u, (SSD_D_INNER, SSD_BC_WIDTH, SSD_BC_WIDTH))
        bsz, ln = u.shape[0], u.shape[1]
        xs = xs.reshape(bsz, ln, SSD_HEADS, SSD_HEAD_DIM)
        bm = bm.reshape(bsz, ln, SSD_GROUPS, SSD_D_STATE)
        cm = cm.reshape(bsz, ln, SSD_GROUPS, SSD_D_STATE)
        dt = jax.nn.softplus(dt_t.reshape(bsz, ln, 2, SSD_HEADS).astype(jnp.float32) + dt_bias.astype(jnp.float32))
        return xs, bm, cm, dt

    xl, bl, cl, dtl = prep(xbc, dt_raw)
    xc, bc, cc, dtc = prep(xbcc, dtc_raw)
    zero = jnp.zeros((xc.shape[0], SSD_HEADS, SSD_HEAD_DIM, SSD_D_STATE), jnp.float32)
    rev = lambda t: t[:, ::-1]
    yc_f, hc_f = _ssd_chunked(xc, dtc[:, :, 0], a[0], bc, cc, zero)
    yl_f, _ = _ssd_chunked(xl, dtl[:, :, 0], a[0], bl, cl, hc_f)
    yc_b, hc_b = _ssd_chunked(rev(xc), rev(dtc[:, :, 1]), a[1], rev(bc), rev(cc), zero)
    yl_b, _ = _ssd_chunked(rev(xl), rev(dtl[:, :, 1]), a[1], rev(bl), rev(cl), hc_b)

    def finish(y_f, y_b, xs, zt):
        y = y_f + rev(y_b) + d_skip.astype(jnp.float32)[:, None] * xs.astype(jnp.float32)
        y = y.reshape(y.shape[0], y.shape[1], SSD_D_INNER)
        return _gated_rmsnorm(y, zt, ssd_norm).astype(zt.dtype)

    out_l = finish(yl_f, yl_b, xl, z)
    out_c = finish(yc_f, yc_b, xc, zc) if with_ctx_out else None
    return out_l, out_c


def _hybrid_layer(x, xc, mod, mod_c, norm_mix, norm_ffn, w_in, rpb, conv_w, conv_b, dt_bias, a_log, d_skip,
                  ssd_norm, w_br_na, w_br_ssd, w_out, w_gate_up, w_down, with_ctx_out):
    sh_a, sc_a, g_a, sh_f, sc_f, g_f = jnp.split(mod, 6, axis=-1)
    csh_a, csc_a, cg_a, csh_f, csc_f, cg_f = jnp.split(mod_c, 6, axis=-1)
    h = _modulate(_rmsnorm(x, norm_mix), sh_a, sc_a)
    hc = _modulate(_rmsnorm(xc, norm_mix), csh_a, csc_a)
    q, k, v, z, xbc, dt, g_na, g_ssd = _split_cols(h @ w_in, IN_SPLITS)
    qc, kc, vc, zc, xbcc, dtc, g_nac, g_ssdc = _split_cols(hc @ w_in, IN_SPLITS)
    heads = lambda t: t.reshape(t.shape[0], t.shape[1], NA_HEADS, NA_HEAD_DIM)

    attn_l = _neighbourhood_attention(heads(q), heads(k), heads(v), heads(kc), heads(vc), rpb)
    ssd_l, ssd_c = _bidirectional_ssd(z, xbc, dt, zc, xbcc, dtc, conv_w, conv_b, dt_bias, a_log, d_skip,
                                      ssd_norm, with_ctx_out)

    def merge(attn_o, ssd_o, gate1, gate2):
        a_o = attn_o.reshape(attn_o.shape[0], attn_o.shape[1], NA_WIDTH) @ w_br_na
        s_o = ssd_o @ w_br_ssd
        return (jax.nn.sigmoid(gate1) * a_o + jax.nn.sigmoid(gate2) * s_o) @ w_out

    def ffn(t, shift, scale):
        u = _modulate(_rmsnorm(t, norm_ffn), shift, scale)
        gate, up = jnp.split(u @ w_gate_up, 2, axis=-1)
        return (jax.nn.silu(gate) * up) @ w_down

    x = x + g_a * merge(attn_l, ssd_l, g_na, g_ssd)
    x = x + g_f * ffn(x, sh_f, sc_f)
    if with_ctx_out:
        attn_c = _context_attention(heads(qc), heads(kc), heads(vc))
        xc = xc + cg_a * merge(attn_c, ssd_c, g_nac, g_ssdc)
        xc = xc + cg_f * ffn(xc, csh_f, csc_f)
    return x, xc


def setup_inputs(seed: int = 0) -> dict:
    key = jax.random.key(seed)
    ks = jax.random.split(key, 24)
    f32 = jnp.float32
    L = DEPTH
    nrm = lambda k, shape, s: jax.random.normal(k, shape, f32) * s
    dt0 = jnp.exp(jax.random.uniform(ks[12], (L, 2, SSD_HEADS), f32, math.log(1e-3), math.log(1e-1)))
    return {
        'x': nrm(ks[0], (BATCH, SEQ, D_MODEL), 1.0),
        'c': nrm(ks[1], (BATCH, D_MODEL), 1.0),
        'ctx': nrm(ks[2], (BATCH, CTX_LEN, D_MODEL), 1.0),
        'c_ctx': nrm(ks[3], (D_MODEL,), 1.0),
        'w_ada': nrm(ks[4], (L, D_MODEL, 6 * D_MODEL), 0.5 * D_MODEL ** -0.5),
        'b_ada': nrm(ks[5], (L, 6 * D_MODEL), 0.02),
        'norm_mix': 1.0 + nrm(ks[6], (L, D_MODEL), 0.02),
        'norm_ffn': 1.0 + nrm(ks[7], (L, D_MODEL), 0.02),
        'w_in': nrm(ks[8], (L, D_MODEL, IN_PROJ_DIM), D_MODEL ** -0.5),
        'na_rpb': nrm(ks[9], (L, NA_HEADS, 2 * NA_WIN_H - 1, 2 * NA_WIN_W - 1), 0.1),
        'conv_w': nrm(ks[10], (L, SSD_CONV, SSD_CONV_DIM), SSD_CONV ** -0.5),
        'conv_b': nrm(ks[11], (L, SSD_CONV_DIM), 0.02),
        'dt_bias': dt0 + jnp.log(-jnp.expm1(-dt0)),
        'a_log': jnp.log(jax.random.uniform(ks[13], (L, 2, SSD_HEADS), f32, 1.0, 16.0)),
        'd_skip': 1.0 + nrm(ks[14], (L, SSD_HEADS), 0.1),
        'ssd_norm': 1.0 + nrm(ks[15], (L, SSD_D_INNER), 0.02),
        'w_br_na': nrm(ks[16], (L, NA_WIDTH, D_MODEL), NA_WIDTH ** -0.5),
        'w_br_ssd': nrm(ks[17], (L, SSD_D_INNER, D_MODEL), SSD_D_INNER ** -0.5),
        'w_out': nrm(ks[18], (L, D_MODEL, D_MODEL), D_MODEL ** -0.5),
        'w_gate_up': nrm(ks[19], (L, D_MODEL, 2 * D_FF), D_MODEL ** -0.5),
        'w_down': nrm(ks[20], (L, D_FF, D_MODEL), D_FF ** -0.5),
        'norm_final': 1.0 + nrm(ks[21], (D_MODEL,), 0.02),
    }


def reference(x, c, ctx, c_ctx, w_ada, b_ada, norm_mix, norm_ffn, w_in, na_rpb, conv_w, conv_b, dt_bias, a_log,
              d_skip, ssd_norm, w_br_na, w_br_ssd, w_out, w_gate_up, w_down, norm_final):
    xc = ctx
    silu_c = jax.nn.silu(c)
    silu_cc = jax.nn.silu(c_ctx)
    for i in range(DEPTH):
        mod = (silu_c @ w_ada[i] + b_ada[i])[:, None, :]
        mod_c = (silu_cc @ w_ada[i] + b_ada[i])[None, None, :]
        x, xc = _hybrid_layer(x, xc, mod, mod_c, norm_mix[i], norm_ffn[i], w_in[i], na_rpb[i], conv_w[i],
                              conv_b[i], dt_bias[i], a_log[i], d_skip[i], ssd_norm[i], w_br_na[i], w_br_ssd[i],
                              w_out[i], w_gate_up[i], w_down[i], i < DEPTH - 1)
    return _rmsnorm(x, norm_final)
```

```python
import contextlib
import numpy as np
import concourse.bass as bass
import concourse.mybir as mybir
from concourse.bass_utils import run_bass_kernel_spmd

F32 = mybir.dt.float32
BF16 = mybir.dt.bfloat16
AF = mybir.ActivationFunctionType
ALU = mybir.AluOpType

D = 2048
NCTX = 256
NLAT = 4096
NTOK = NCTX + NLAT
DEPTH = 4
NIN = 20608
DFF = 5632
EPS = 1e-6
NEG = -30000.0
SCALE = 128 ** -0.5
NCHAN = 8
SAME_ENGINE_SYNC = True


class Slot:
    __slots__ = ("name", "w", "r")

    def __init__(self, name):
        self.name = name
        self.w = None
        self.r = {}


class Op:
    __slots__ = ("eng", "fn", "waits", "stream", "seq")


class Prog:
    ENGS = ("pe", "act", "dve", "pool", "sp")

    def __init__(self, nc):
        self.nc = nc
        self.ops = {e: [] for e in self.ENGS}
        self.count = {}
        self.known = {e: {} for e in self.ENGS}
        self.needinc = {}
        self.chan_rr = {e: 0 for e in self.ENGS}
        self.nslots = 0

    def slot(self, name=None):
        self.nslots += 1
        return Slot(name or f"s{self.nslots}")

    def slots(self, n, name="s"):
        return [self.slot(f"{name}{i}") for i in range(n)]

    def _deps(self, eng, stream, reads, writes):
        deps = {}

        def add(d):
            if d is None:
                return
            st, sq = d
            if st == stream and "." in st:
                return
            if st == eng and (eng == "pe" or not SAME_ENGINE_SYNC):
                return
            if deps.get(st, 0) < sq:
                deps[st] = sq
        for s in reads:
            add(s.w)
        for s in writes:
            add(s.w)
            for st, sq in s.r.items():
                add((st, sq))
        out = []
        kn = self.known[eng]
        for st, sq in deps.items():
            if kn.get(st, 0) >= sq:
                continue
            kn[st] = sq
            out.append((st, sq))
            self.needinc.setdefault(st, set()).add(sq)
        return out

    def _commit(self, stream, seq, reads, writes):
        for s in reads:
            s.r[stream] = seq
        for s in writes:
            s.w = (stream, seq)
            s.r = {}

    def op(self, eng, fn, reads=(), writes=()):
        seq = self.count.get(eng, 0) + 1
        self.count[eng] = seq
        o = Op()
        o.eng = eng
        o.fn = fn
        o.stream = eng
        o.seq = seq
        o.waits = self._deps(eng, eng, reads, writes)
        self.ops[eng].append(o)
        self._commit(eng, seq, reads, writes)
        return o

    def dma(self, eng, out, in_, reads=(), writes=()):
        c = self.chan_rr[eng]
        self.chan_rr[eng] = (c + 1) % NCHAN
        stream = f"{eng}.ch{c}"
        seq = self.count.get(stream, 0) + 1
        self.count[stream] = seq
        o = Op()
        o.eng = eng
        o.fn = lambda e, out=out, in_=in_: e.dma_start(out=out, in_=in_)
        o.stream = stream
        o.seq = seq
        o.waits = self._deps(eng, stream, reads, writes)
        if seq > 1 and self.known[eng].get(stream, 0) < seq - 1:
            self.known[eng][stream] = seq - 1
            o.waits.append((stream, seq - 1))
        self.ops[eng].append(o)
        self._commit(stream, seq, reads, writes)
        return o

    def barrier(self):
        cur = dict(self.count)
        for eng in self.ENGS:
            o = Op()
            o.eng = eng
            o.fn = None
            o.stream = None
            o.seq = 0
            o.waits = []
            for st, sq in cur.items():
                if st == eng and eng == "pe":
                    continue
                if self.known[eng].get(st, 0) >= sq:
                    continue
                self.known[eng][st] = sq
                o.waits.append((st, sq))
                self.needinc.setdefault(st, set()).add(sq)
            self.ops[eng].append(o)

    def emit(self):
        nc = self.nc
        streams = set(self.count.keys())
        with contextlib.ExitStack() as es:
            sems = {}
            for st in sorted(streams):
                sems[st] = es.enter_context(nc.semaphore("sem_" + st.replace(".", "_")))
            val = {}
            for st in streams:
                if "." in st:
                    continue
                need = sorted(self.needinc.get(st, ()))
                val[st] = {sq: i + 1 for i, sq in enumerate(need)}
            block = es.enter_context(nc.Block())

            def run(eng):
                def body(e):
                    for o in self.ops[eng]:
                        for st, sq in o.waits:
                            if "." in st:
                                e.wait_ge(sems[st], 16 * sq)
                            else:
                                e.wait_ge(sems[st], val[st][sq])
                        if o.fn is None:
                            continue
                        ins = o.fn(e)
                        if "." in o.stream:
                            ins.then_inc(sems[o.stream], 16)
                        elif o.seq in val[o.stream]:
                            ins.then_inc(sems[o.stream], 1)
                return body
            block.tensor(run("pe"))
            block.scalar(run("act"))
            block.vector(run("dve"))
            block.gpsimd(run("pool"))
            block.sync(run("sp"))


class Arena:
    def __init__(self, handle, nelem):
        self.h = handle
        self.n = nelem
        self.off = 0

    def reset(self):
        self.off = 0

    def alloc(self, dt, *shape):
        n = 1
        for s in shape:
            n *= s
        ne = n * 2 if dt == F32 else n
        ne = (ne + 15) // 16 * 16
        assert self.off + ne <= self.n, f"arena overflow {self.off}+{ne}>{self.n}"
        v = self.h[:, self.off:self.off + ne]
        self.off += ne
        if dt == F32:
            v = v.bitcast(F32)
        v = v[:, 0:n]
        if len(shape) == 2:
            v = v.rearrange("p (a b) -> p a b", a=shape[0])
        elif len(shape) == 3:
            v = v.rearrange("p (a b c) -> p a b c", a=shape[0], b=shape[1])
        return v


def fm(ap):
    return ap.rearrange("(c p) t -> p c t", p=128)


class Builder:
    def __init__(self, nlayers=DEPTH, dbg=None):
        self.nlayers = nlayers
        self.dbg = dbg or {}
        self.nc = bass.Bass("TRN2", target_bir_lowering=False)
        self.P = Prog(self.nc)
        self.bank_rr = 0

    def din(self, name, shape, dt=F32):
        return self.nc.dram_tensor(name, list(shape), dt, kind="ExternalInput").ap()

    def dscr(self, name, shape, dt):
        kind = "ExternalOutput" if name in self.dbg else "Internal"
        return self.nc.dram_tensor(name, list(shape), dt, kind=kind).ap()

    def build(self):
        nc = self.nc
        P = self.P
        L = DEPTH
        I = {}
        tiny = self.dbg.get("tiny")
        I["x_tok"] = self.din("x_tok", [NTOK, D])
        I["c2"] = self.din("c2", [32, 128])
        I["w_ada"] = self.din("w_ada", [L, 128, 128] if tiny else [L, D, 6 * D])
        I["b_ada"] = self.din("b_ada", [L, 96, 128])
        I["norms"] = self.din("norms", [L, 32, 128])
        I["norm_final"] = self.din("norm_final", [32, 128])
        I["w_in"] = self.din("w_in", [L, 128, 128] if tiny else [L, D, NIN])
        I["rpbT"] = self.din("rpbT", [L, 16, 128, 22 * 64])
        I["conv_w"] = self.din("conv_w", [L, 256, 128])
        I["conv_b"] = self.din("conv_b", [L, 64, 128])
        I["dt_bias"] = self.din("dt_bias", [L, 128])
        I["a_log"] = self.din("a_log", [L, 128])
        I["d_skip"] = self.din("d_skip", [L, 64])
        I["ssd_norm"] = self.din("ssd_norm", [L, 4096])
        I["w_br_na"] = self.din("w_br_na", [L, 128, 128] if tiny else [L, D, D])
        I["w_br_ssd"] = self.din("w_br_ssd", [L, 128, 128] if tiny else [L, 2 * D, D])
        I["w_out"] = self.din("w_out", [L, 128, 128] if tiny else [L, D, D])
        I["w_gate_up"] = self.din("w_gate_up", [L, 128, 128] if tiny else [L, D, 2 * DFF])
        I["w_down"] = self.din("w_down", [L, 128, 128] if tiny else [L, DFF, D])
        I["cmask"] = self.din("cmask", [5, 128, 128])
        I["bmask"] = self.din("bmask", [2, 3 * 8 * 512])
        I["a2"] = self.din("a2", [2, 128])
        self.I = I
        self.out = nc.dram_tensor("out", [NLAT, D], F32, kind="ExternalOutput").ap()
        Sx = {}
        Sx["xT"] = self.dscr("xT", [D, NTOK], F32)
        Sx["qkT"] = self.dscr("qkT", [2 * D, NTOK], BF16)
        Sx["vTM"] = self.dscr("vTM", [NTOK, D], BF16)
        Sx["zTM"] = self.dscr("zTM", [NTOK, 2 * D], BF16)
        Sx["xbcT"] = self.dscr("xbcT", [6144, NTOK], BF16)
        Sx["dtTM"] = self.dscr("dtTM", [NTOK, 128], F32)
        Sx["gT"] = self.dscr("gT", [2 * D, NTOK], BF16)
        Sx["uBCT"] = self.dscr("uBCT", [2048, NTOK], BF16)
        Sx["xsB"] = self.dscr("xsB", [NTOK, 5120], BF16)
        Sx["yF"] = self.dscr("yF", [NTOK, 4096], BF16)
        Sx["ssdT"] = self.dscr("ssdT", [2 * D, NTOK], BF16)
        Sx["attnT"] = self.dscr("attnT", [D, NTOK], BF16)
        Sx["mT"] = self.dscr("mT", [D, NTOK], BF16)
        Sx["actT"] = self.dscr("actT", [DFF, NTOK], BF16)
        self.S = Sx
        self.ds = {k: P.slot("d_" + k) for k in Sx}
        self.ds["out"] = P.slot("d_out")

        with contextlib.ExitStack() as es:
            def sb(name, shape, dt=F32):
                return es.enter_context(nc.sbuf_tensor(name, list(shape), dt))
            self.ps = [es.enter_context(nc.psum_tensor(f"ps{i}", [128, 512], F32)) for i in range(8)]
            self.pss = P.slots(8, "ps")
            self.cm = sb("cm", [128, 5, 128])
            self.cmb = sb("cmb", [128, 5, 128], BF16)
            self.onesb = sb("onesb", [128, 128], BF16)
            self.meanb = sb("meanb", [128, 128], BF16)
            self.onesf = sb("onesf", [128, 128])
            self.epsb = sb("epsb", [128, 1])
            self.scT = sb("scT", [128, 2, 16])
            self.modT = sb("modT", [128, L, 96, 2])
            self.par = sb("par", [128, 6, 16, 2])
            self.nfin = sb("nfin", [128, 32])
            self.cw = sb("cw", [128, 256])
            self.cb = sb("cb", [128, 64])
            self.dtb = sb("dtb", [128, 128])
            self.aneg = sb("aneg", [128, 128])
            self.dsk = sb("dsk", [128, 64])
            self.a2 = sb("a2sb", [2, 128], BF16)
            ARENA = 98 * 1024
            self.arena_h = sb("arena", [128, ARENA], BF16)
            self.A = Arena(self.arena_h, ARENA)
            self.s_const = P.slot("const")
            self.s_par = P.slot("par")

            self.prologue()
            for l in range(self.nlayers):
                self.layer(l)
            if self.on("final"):
                self.final()
            P.barrier()
            P.emit()
        return nc

    def bank(self):
        i = self.bank_rr
        self.bank_rr = (i + 1) % 8
        return self.ps[i], self.pss[i]

    def phase(self):
        self.P.barrier()
        self.A.reset()

    def act(self, out, in_, func, reads, writes, **kw):
        self.P.op("act", lambda e: e.activation(out=out, in_=in_, func=func, **kw), reads, writes)

    def mm(self, out, lhsT, rhs, start, stop, reads, writes):
        self.P.op("pe", lambda e: e.matmul(out, lhsT, rhs, start=start, stop=stop), reads, writes)

    def tt(self, eng, out, in0, in1, op, reads, writes):
        self.P.op(eng, lambda e: e.tensor_tensor(out=out, in0=in0, in1=in1, op=op), reads, writes)

    def ts(self, eng, out, in0, s1, s2, op0, op1, reads, writes):
        self.P.op(eng, lambda e: e.tensor_scalar(out=out, in0=in0, scalar1=s1, scalar2=s2, op0=op0, op1=op1),
                  reads, writes)

    def stt(self, eng, out, in0, scalar, in1, op0, op1, reads, writes):
        self.P.op(eng, lambda e: e.scalar_tensor_tensor(out=out, in0=in0, scalar=scalar, in1=in1, op0=op0, op1=op1),
                  reads, writes)

    def cp(self, eng, out, in_, reads, writes):
        if eng == "act":
            self.act(out, in_, AF.Copy, reads, writes)
        else:
            self.P.op(eng, lambda e: e.tensor_copy(out=out, in_=in_), reads, writes)

    def tr(self, out, in_, ident, reads, writes):
        self.P.op("pe", lambda e: e.transpose(out, in_, ident), reads, writes)

    def small_T(self, src_dram, rows, dst, dst_slot):
        P = self.P
        A = self.A
        t = A.alloc(F32, 128)
        s = P.slot()
        P.dma("sp", t[0:rows, :], src_dram, writes=[s])
        pb, pbs = self.bank()
        self.tr(pb[:, 0:rows], t[0:rows, :], self.cm[0:rows, 0, 0:rows], [s, self.s_const], [pbs])
        self.cp("dve", dst, pb[:, 0:rows], [pbs], [dst_slot])

    def prologue(self):
        P, A, I = self.P, self.A, self.I
        sc = self.s_const
        P.dma("sp", self.cm[:], I["cmask"].rearrange("k p f -> p k f"), writes=[sc])
        P.op("dve", lambda e: e.tensor_copy(out=self.cmb[:], in_=self.cm[:]), [sc], [sc])
        P.op("pool", lambda e: e.memset(self.onesb[:], 1.0), [], [sc])
        P.op("pool", lambda e: e.memset(self.meanb[:], 1.0 / D), [], [sc])
        P.op("pool", lambda e: e.memset(self.onesf[:], 1.0), [], [sc])
        P.op("pool", lambda e: e.memset(self.epsb[:], EPS), [], [sc])
        P.dma("pool", self.a2[:], I["a2"], writes=[sc])
        self.small_T(I["norm_final"], 32, self.nfin[:], sc)
        c2raw = A.alloc(F32, 32)
        s_c2 = P.slot()
        self.small_T(I["c2"], 32, c2raw, s_c2)
        self.act(self.scT[:].rearrange("p r c -> p (r c)"), c2raw, AF.Silu, [s_c2], [sc])
        self.phase()
        if not self.on("prologue"):
            P.op("pool", lambda e: e.memset(self.modT[:], 0.1), [], [sc])
            return
        xT = fm(self.S["xT"])
        xin = [A.alloc(F32, D) for _ in range(2)]
        s_xin = P.slots(2)
        stg = [A.alloc(F32, 16, 512) for _ in range(1)]
        s_stg = P.slots(1)
        for c in range(NTOK // 128):
            b = c % 2
            P.dma("sp", xin[b], I["x_tok"][c * 128:(c + 1) * 128, :], writes=[s_xin[b]])
            for q in range(4):
                pb, pbs = self.bank()
                for k in range(4):
                    fc = q * 4 + k
                    self.tr(pb[:, k * 128:(k + 1) * 128], xin[b][:, fc * 128:(fc + 1) * 128], self.cm[:, 0, :],
                            [s_xin[b], sc], [pbs])
                eng = "act" if q % 2 == 0 else "dve"
                self.cp(eng, stg[0][:, q * 4:(q + 1) * 4, (c % 4) * 128:(c % 4 + 1) * 128],
                        pb[:].rearrange("p (k t) -> p k t", k=4), [pbs], [s_stg[0]])
            if c % 4 == 3 or c == NTOK // 128 - 1:
                t0 = (c // 4) * 512
                wv = (c % 4 + 1) * 128
                P.dma("sp", xT[:, :, t0:t0 + wv], stg[0][:, :, 0:wv], reads=[s_stg[0]], writes=[self.ds["xT"]])
        self.phase()
        wsl = [A.alloc(F32, 16, 512) for _ in range(2)]
        s_w = P.slots(2)
        badaT = A.alloc(F32, 96)
        s_b = P.slot()
        it = 0
        for l in range(self.nlayers):
            self.small_T(I["b_ada"][l], 96, badaT, s_b)
            pb, pbs = self.bank()
            for s in range(24):
                b = it % 2
                it += 1
                P.dma("sp", wsl[b], fm(I["w_ada"][l])[:, :, s * 512:(s + 1) * 512], writes=[s_w[b]])
                for n in range(4):
                    ch = s * 4 + n
                    for kc in range(16):
                        self.mm(pb[:, ch * 2:ch * 2 + 2], wsl[b][:, kc, n * 128:(n + 1) * 128], self.scT[:, :, kc],
                                kc == 0, kc == 15, [s_w[b], sc], [pbs])
            self.tt("dve", self.modT[:, l, :, :], pb[:, 0:192].rearrange("p (c r) -> p c r", r=2),
                    badaT.unsqueeze(2).to_broadcast([128, 96, 2]), ALU.add, [pbs, s_b], [sc])
        self.phase()

    def layer_params(self, l):
        P, A, I = self.P, self.A, self.I
        sp_ = self.s_par
        sc = self.s_const
        nmf = A.alloc(F32, 32)
        nm = nmf[:, 0:16]
        nf = nmf[:, 16:32]
        s_n = P.slot()
        self.small_T(I["norms"][l], 32, nmf, s_n)
        if "dbg_nmf" in self.dbg:
            dn = self.nc.dram_tensor("dbg_nmf", [128, 32], F32, kind="ExternalOutput").ap()
            P.dma("sp", dn, nmf, reads=[s_n], writes=[P.slot()])
        mod = self.modT
        for (k, nrm, sh, scl, g) in ((0, nm, 0, 1, 2), (3, nf, 3, 4, 5)):
            self.stt("dve", self.par[:, k, :, :], mod[:, l, scl * 16:(scl + 1) * 16, :], 1.0,
                     nrm.unsqueeze(2).to_broadcast([128, 16, 2]), ALU.add, ALU.mult, [sc, s_n], [sp_])
            self.cp("dve", self.par[:, k + 1, :, :], mod[:, l, sh * 16:(sh + 1) * 16, :], [sc], [sp_])
            self.cp("dve", self.par[:, k + 2, :, :], mod[:, l, g * 16:(g + 1) * 16, :], [sc], [sp_])
        self.small_T(I["conv_w"][l, 0:128, :], 128, self.cw[:, 0:128], sp_)
        self.small_T(I["conv_w"][l, 128:256, :], 128, self.cw[:, 128:256], sp_)
        self.small_T(I["conv_b"][l], 64, self.cb[:], sp_)
        P.dma("sp", self.dtb[:], I["dt_bias"][l].partition_broadcast(128), writes=[sp_])
        P.dma("sp", self.aneg[:], I["a_log"][l].partition_broadcast(128), writes=[sp_])
        P.dma("sp", self.dsk[:], I["d_skip"][l].partition_broadcast(128), writes=[sp_])
        self.act(self.aneg[:], self.aneg[:], AF.Exp, [sp_], [sp_])
        self.ts("dve", self.aneg[:], self.aneg[:], -1.0, 0.0, ALU.mult, ALU.add, [sp_], [sp_])

    def norm_group(self, hT, s_h, tok0, ntok, pk):
        P, A = self.P, self.A
        xT = fm(self.S["xT"])
        xt = [A.alloc(F32, 16, 256) for _ in range(2)]
        s_xt = P.slots(2)
        sq = [A.alloc(BF16, 256) for _ in range(2)]
        s_sq = P.slots(2)
        rstd = A.alloc(F32, 256)
        s_r = P.slot()
        tmp = [A.alloc(F32, 256) for _ in range(2)]
        s_tmp = P.slots(2)
        for ti in range(ntok // 256):
            t0 = tok0 + ti * 256
            col = 1 if t0 < NCTX else 0
            b = ti % 2
            P.dma("sp", xt[b], xT[:, :, t0:t0 + 256], reads=[self.ds["xT"]], writes=[s_xt[b]])
            pb, pbs = self.bank()
            for c in range(16):
                self.act(sq[c % 2], xt[b][:, c, :], AF.Square, [s_xt[b]], [s_sq[c % 2]])
                self.mm(pb[:, 0:256], self.meanb[:], sq[c % 2], c == 0, c == 15, [s_sq[c % 2], self.s_const], [pbs])
            self.act(rstd, pb[:, 0:256], AF.Sqrt, [pbs, self.s_const], [s_r], bias=self.epsb[:])
            self.P.op("dve", lambda e, rstd=rstd: e.reciprocal(out=rstd, in_=rstd), [s_r], [s_r])
            for c in range(16):
                self.stt("dve", tmp[c % 2], xt[b][:, c, :], self.par[:, pk, c, col:col + 1], rstd, ALU.mult, ALU.mult,
                         [s_xt[b], self.s_par, s_r], [s_tmp[c % 2]])
                self.act(hT[:, c, ti * 256:(ti + 1) * 256], tmp[c % 2], AF.Identity, [s_tmp[c % 2], self.s_par], [s_h],
                         bias=self.par[:, pk + 1, c, col:col + 1])

    def gemm_fm(self, inT, s_in, KC, ntok, W, col0, ncols, slab_w, wbufs, s_wb, epi, tile_w=512):
        P = self.P
        Wv = fm(W)
        nslab = (ncols + slab_w - 1) // slab_w
        for s in range(nslab):
            c0 = col0 + s * slab_w
            w = min(slab_w, col0 + ncols - c0)
            b = self.wrr % 2
            self.wrr += 1
            wb = wbufs[b]
            P.dma("pool", wb[:, 0:KC, 0:w], Wv[:, :, c0:c0 + w], writes=[s_wb[b]])
            for n in range(w // 128):
                ch = (c0 - col0) // 128 + n
                t0 = 0
                while t0 < ntok:
                    tw = min(tile_w, ntok - t0)
                    pb, pbs = self.bank()
                    for kc in range(KC):
                        self.mm(pb[:, 0:tw], wb[:, kc, n * 128:(n + 1) * 128], inT[:, kc, t0:t0 + tw], kc == 0, kc == KC - 1,
                                [s_wb[b], s_in], [pbs])
                    epi(pb, pbs, ch, t0, tw)
                    t0 += tw

    def gemm_tm(self, inT, s_in, KC, ntok, W, col0, ncols, slab_w, wbufs, s_wb, epi):
        P = self.P
        Wv = fm(W)
        nslab = (ncols + slab_w - 1) // slab_w
        for s in range(nslab):
            c0 = col0 + s * slab_w
            w = min(slab_w, col0 + ncols - c0)
            b = self.wrr % 2
            self.wrr += 1
            wb = wbufs[b]
            P.dma("pool", wb[:, 0:KC, 0:w], Wv[:, :, c0:c0 + w], writes=[s_wb[b]])
            for c in range(ntok // 128):
                pb, pbs = self.bank()
                for kc in range(KC):
                    self.mm(pb[:, 0:w], inT[:, kc, c * 128:(c + 1) * 128], wb[:, kc, 0:w], kc == 0, kc == KC - 1,
                            [s_wb[b], s_in], [pbs])
                epi(pb, pbs, c0 - col0, w, c)

    def phase_inproj(self, l):
        P, A, I, S = self.P, self.A, self.I, self.S
        self.phase()
        self.wrr = 0
        GM = 2304
        hT = A.alloc(BF16, 16, GM)
        s_h = P.slot()
        wb = [A.alloc(BF16, 16, 512) for _ in range(2)]
        s_wb = P.slots(2)
        stg = [A.alloc(BF16, GM) for _ in range(2)]
        s_stg = P.slots(2)
        stt_ = [A.alloc(BF16, 512) for _ in range(2)]
        s_stt = P.slots(2)
        stf = [A.alloc(F32, 128) for _ in range(2)]
        s_stf = P.slots(2)
        W = I["w_in"][l]
        self.ev = 0
        for (g0, G) in ((0, 2304), (2304, 2048)):
            self.norm_group(hT, s_h, g0, G, 0)
            if "dbg_hT" in self.dbg and g0 == 0:
                dh = self.nc.dram_tensor("dbg_hT", [128, 16, 2304], BF16, kind="ExternalOutput").ap()
                dm = self.nc.dram_tensor("dbg_mod", [128, DEPTH * 96 * 2], F32, kind="ExternalOutput").ap()
                dp = self.nc.dram_tensor("dbg_par", [128, 6 * 16 * 2], F32, kind="ExternalOutput").ap()
                sd = P.slot()
                P.dma("sp", dh, hT[:, :, 0:2304], reads=[s_h], writes=[sd])
                P.dma("sp", dm, self.modT[:].rearrange("p a b c -> p (a b c)"), reads=[self.s_const], writes=[sd])
                P.dma("sp", dp, self.par[:].rearrange("p a b c -> p (a b c)"), reads=[self.s_par], writes=[sd])

            def fm_epi(dst, dname, rowbase, g0=g0, G=G):
                def epi(pb, pbs, ch, t0, tw):
                    b = ch % 2
                    eng = "act" if self.ev % 2 == 0 else "dve"
                    self.ev += 1
                    self.cp(eng, stg[b][:, t0:t0 + tw], pb[:, 0:tw], [pbs], [s_stg[b]])
                    if t0 + tw == G:
                        r0 = rowbase + ch * 128
                        P.dma("sp", dst[r0:r0 + 128, g0:g0 + G], stg[b][:, 0:G], reads=[s_stg[b]], writes=[self.ds[dname]])
                return epi

            def tm_epi(dst, dname, colbase, g0=g0):
                def epi(pb, pbs, coff, w, c):
                    b = self.ev % 2
                    eng = "act" if self.ev % 2 == 0 else "dve"
                    self.ev += 1
                    self.cp(eng, stt_[b][:, 0:w], pb[:, 0:w], [pbs], [s_stt[b]])
                    t0 = g0 + c * 128
                    P.dma("sp", dst[t0:t0 + 128, colbase + coff:colbase + coff + w], stt_[b][:, 0:w], reads=[s_stt[b]],
                          writes=[self.ds[dname]])
                return epi

            def dt_epi(pb, pbs, coff, w, c, g0=g0):
                b = self.ev % 2
                self.ev += 1
                self.cp("dve", stf[b], pb[:, 0:128], [pbs], [s_stf[b]])
                t0 = g0 + c * 128
                P.dma("sp", S["dtTM"][t0:t0 + 128, :], stf[b], reads=[s_stf[b]], writes=[self.ds["dtTM"]])

            self.gemm_fm(hT, s_h, 16, G, W, 0, 4096, 512, wb, s_wb, fm_epi(S["qkT"], "qkT", 0))
            self.gemm_tm(hT, s_h, 16, G, W, 4096, 2048, 512, wb, s_wb, tm_epi(S["vTM"], "vTM", 0))
            self.gemm_tm(hT, s_h, 16, G, W, 6144, 4096, 512, wb, s_wb, tm_epi(S["zTM"], "zTM", 0))
            self.gemm_fm(hT, s_h, 16, G, W, 10240, 6144, 512, wb, s_wb, fm_epi(S["xbcT"], "xbcT", 0))
            self.gemm_tm(hT, s_h, 16, G, W, 16384, 128, 128, wb, s_wb, dt_epi)
            self.gemm_fm(hT, s_h, 16, G, W, 16512, 4096, 512, wb, s_wb, fm_epi(S["gT"], "gT", 0))

    def phase_conv(self, l):
        P, A, S = self.P, self.A, self.S
        self.phase()
        xbc = fm(S["xbcT"])
        uT = fm(S["uBCT"])
        NG = 8
        xin = [A.alloc(BF16, NG, 516) for _ in range(2)]
        s_xin = P.slots(2)
        acc = [A.alloc(F32, 512) for _ in range(4)]
        s_acc = P.slots(4)
        u = [A.alloc(BF16, NG, 512) for _ in range(2)]
        s_u = P.slots(2)
        xst = [A.alloc(BF16, 4, 1024) for _ in range(2)]
        s_xst = P.slots(2)
        tiles = [(0, 256)] + [(256 + 512 * i, 512) for i in range(8)]
        it = 0
        for (t0, w) in tiles:
            seq0 = 0 if t0 < NCTX else NCTX
            seq1 = NCTX if t0 < NCTX else NTOK
            for cg in range(48 // NG):
                b = it % 2
                it += 1
                lo = max(t0 - 2, seq0)
                hi = min(t0 + w + 2, seq1)
                if lo > t0 - 2:
                    P.op("pool", lambda e, b=b: e.memset(xin[b][:, :, 0:2], 0.0), [], [s_xin[b]])
                if hi < t0 + w + 2:
                    P.op("pool", lambda e, b=b, w=w: e.memset(xin[b][:, :, w + 2:w + 4], 0.0), [], [s_xin[b]])
                P.dma("sp", xin[b][:, :, lo - (t0 - 2):hi - (t0 - 2)], xbc[:, cg * NG:(cg + 1) * NG, lo:hi],
                      reads=[self.ds["xbcT"]], writes=[s_xin[b]])
                for k in range(NG):
                    cc = cg * NG + k
                    eng = "dve"
                    a = acc[cc % 4]
                    sa = s_acc[cc % 4]
                    self.ts(eng, a[:, 0:w], xin[b][:, k, 0:w], self.cw[:, cc:cc + 1], self.cb[:, cc:cc + 1], ALU.mult, ALU.add,
                            [s_xin[b], self.s_par], [sa])
                    for kk in range(1, 5):
                        self.stt(eng, a[:, 0:w], xin[b][:, k, kk:kk + w], self.cw[:, kk * 48 + cc:kk * 48 + cc + 1], a[:, 0:w],
                                 ALU.mult, ALU.add, [s_xin[b], self.s_par, sa], [sa])
                    self.act(u[b][:, k, 0:w], a[:, 0:w], AF.Silu, [sa], [s_u[b]])
                if cg >= 4:
                    P.dma("sp", uT[:, (cg - 4) * NG:(cg - 3) * NG, t0:t0 + w], u[b][:, :, 0:w], reads=[s_u[b]],
                          writes=[self.ds["uBCT"]])
                if cg < 5:
                    sb_ = it % 2
                    for tcn in range(w // 128):
                        for half in range(2):
                            pb, pbs = self.bank()
                            pbb = pb[:].bitcast(BF16)
                            for k in range(4):
                                kk = half * 4 + k
                                self.tr(pbb[:, k * 128:(k + 1) * 128], u[b][:, kk, tcn * 128:(tcn + 1) * 128], self.cmb[:, 0, :],
                                        [s_u[b], self.s_const], [pbs])
                            eng = "act" if half == 0 else "dve"
                            self.cp(eng, xst[sb_][:, tcn, half * 512:(half + 1) * 512], pbb[:, 0:512], [pbs], [s_xst[sb_]])
                    P.dma("sp", S["xsB"][t0:t0 + w, cg * 1024:(cg + 1) * 1024].rearrange("(c p) n -> p c n", p=128),
                          xst[sb_][:, 0:w // 128, :], reads=[s_xst[sb_]], writes=[self.ds["xsB"]])

    def phase_ssd(self, l, d):
        P, A, S, I = self.P, self.A, self.S, self.I
        self.phase()
        sc = self.s_const
        TRI = 1 + d
        LST = 3 + d
        order = list(range(34)) if d == 0 else [1, 0] + list(range(33, 1, -1))
        xs = [A.alloc(BF16, 5120) for _ in range(2)]
        s_xs = P.slots(2)
        bc = [A.alloc(BF16, 16, 128) for _ in range(2)]
        s_bc = P.slots(2)
        dtr = [A.alloc(F32, 128) for _ in range(2)]
        s_dtr = P.slots(2)
        R = A.alloc(F32, 4096)
        Rb = A.alloc(BF16, 4096)
        s_R = P.slot()
        s_Rb = P.slot()
        sm = A.alloc(F32, 8, 64)
        s_sm = P.slots(8)
        cst = A.alloc(F32, 128)
        s_cst = P.slot()
        xdt = A.alloc(BF16, 4096)
        xdd = A.alloc(BF16, 4096)
        s_xdt = P.slot()
        s_xdd = P.slot()
        cbm = A.alloc(BF16, 8, 128)
        s_cbm = P.slot()
        rhsA = A.alloc(BF16, 64, 128)
        s_rhsA = P.slot()
        Eb = [A.alloc(BF16, 512) for _ in range(2)]
        s_Eb = P.slots(2)
        MT = A.alloc(BF16, 64, 128)
        s_MT = P.slots(16)
        ych = [A.alloc(F32, 4096) for _ in range(1)]
        s_y = P.slots(1)
        ytmp = [A.alloc(F32, 512) for _ in range(2)]
        s_ytmp = P.slots(2)
        yo = [A.alloc(BF16, 4096) for _ in range(2 - d)] * (1 + d)
        s_yo = P.slots(2 - d) * (1 + d)
        if d == 1:
            yf = [A.alloc(BF16, 4096)] * 2
            s_yf = P.slots(1) * 2
            zc = [A.alloc(BF16, 4096) for _ in range(2)]
            s_zc = P.slots(2)
            nw = A.alloc(BF16, 4096)
            s_nw = P.slot()
            P.dma("pool", nw, I["ssd_norm"][l].partition_broadcast(128), writes=[s_nw])
            gss = A.alloc(F32, 8)
            s_gss = P.slot()
            gtmp = A.alloc(F32, 4096)
            s_gtmp = P.slot()
            sqj = A.alloc(BF16, 512)
            s_sqj = P.slot()
            ost = [A.alloc(BF16, 32, 128)] * 2
            s_ost = P.slots(1) * 2
            ssdT = fm(S["ssdT"])
        P.op("pool", lambda e: e.memset(R, 0.0), [], [s_R])
        P.op("pool", lambda e: e.memset(Rb, 0.0), [], [s_Rb])
        uT = fm(S["uBCT"])

        def load(i):
            c = order[i]
            b = i % 2
            t0 = c * 128
            P.dma("sp", xs[b], S["xsB"][t0:t0 + 128, :], reads=[self.ds["xsB"]], writes=[s_xs[b]])
            P.dma("sp", bc[b], uT[:, :, t0:t0 + 128], reads=[self.ds["uBCT"]], writes=[s_bc[b]])
            P.dma("sp", dtr[b], S["dtTM"][t0:t0 + 128, :], reads=[self.ds["dtTM"]], writes=[s_dtr[b]])
            if d == 1:
                P.dma("sp", zc[b], S["zTM"][t0:t0 + 128, :], reads=[self.ds["zTM"]], writes=[s_zc[b]])
        load(0)
        for i in range(34):
            c = order[i]
            b = i % 2
            t0 = c * 128
            if i + 1 < 34:
                load(i + 1)
            dsl = slice(d * 64, (d + 1) * 64)
            sub = self.dbg.get("ssd_sub", 99)
            if sub < 1:
                continue
            self.tt("dve", sm[:, 0, :], dtr[b][:, dsl], self.dtb[:, dsl], ALU.add, [s_dtr[b], self.s_par], [s_sm[0]])
            self.act(sm[:, 1, :], sm[:, 0, :], AF.Exp, [s_sm[0]], [s_sm[1]])
            self.act(sm[:, 2, :], sm[:, 1, :], AF.Ln, [s_sm[1], sc], [s_sm[2]], bias=self.onesf[:, 0:1])
            self.tt("dve", sm[:, 3, :], sm[:, 2, :], self.aneg[:, dsl], ALU.mult, [s_sm[2], self.s_par], [s_sm[3]])
            if sub < 2:
                continue
            pc, pcs = self.bank()
            self.mm(pc[:, 0:64], self.cm[:, TRI, :], sm[:, 3, :], True, True, [sc, s_sm[3]], [pcs])
            self.mm(pc[:, 64:128], self.onesf[:], sm[:, 3, :], True, True, [sc, s_sm[3]], [pcs])
            if sub < 3:
                continue
            self.cp("act", cst, pc[:, 0:128], [pcs], [s_cst])
            self.act(sm[:, 4, :], cst[:, 0:64], AF.Exp, [s_cst], [s_sm[4]])
            self.act(sm[:, 5, :], cst[:, 64:128], AF.Exp, [s_cst], [s_sm[5]])
            self.tt("dve", sm[:, 7, :], cst[:, 64:128], cst[:, 0:64], ALU.subtract, [s_cst], [s_sm[7]])
            self.act(sm[:, 6, :], sm[:, 7, :], AF.Exp, [s_sm[7]], [s_sm[6]])
            lvl = self.dbg.get("ssd_level", 99)
            if lvl < 1:
                continue
            x3 = xs[b][:, 0:4096].rearrange("p (h q) -> p h q", h=64)
            self.tt("pool", xdt.rearrange("p (h q) -> p h q", h=64), x3, sm[:, 2, :].unsqueeze(2).to_broadcast([128, 64, 64]),
                    ALU.mult, [s_xs[b], s_sm[2]], [s_xdt])
            self.tt("pool", xdd.rearrange("p (h q) -> p h q", h=64), xdt.rearrange("p (h q) -> p h q", h=64),
                    sm[:, 6, :].unsqueeze(2).to_broadcast([128, 64, 64]), ALU.mult, [s_xdt, s_sm[6]], [s_xdd])
            if lvl < 2:
                continue
            for half in range(2):
                pb, pbs = self.bank()
                for g4 in range(4):
                    g = half * 4 + g4
                    self.mm(pb[:, g4 * 128:(g4 + 1) * 128], bc[b][:, g, :], bc[b][:, 8 + g, :], True, True, [s_bc[b]], [pbs])
                self.tt("dve", cbm[:, half * 4:(half + 1) * 4, :], pb[:].rearrange("p (g l) -> p g l", g=4),
                        self.cm[:, TRI, :].unsqueeze(1).to_broadcast([128, 4, 128]), ALU.mult, [pbs, sc], [s_cbm])
            if lvl < 3:
                continue
            self.tt("pool", rhsA, sm[:, 3, :].unsqueeze(2).to_broadcast([128, 64, 128]),
                    self.cm[:, TRI, :].unsqueeze(1).to_broadcast([128, 64, 128]), ALU.mult, [s_sm[3], sc], [s_rhsA])
            if lvl < 4:
                continue
            for hq in range(16):
                pb, pbs = self.bank()
                self.mm(pb[:], self.cmb[:, LST, :], rhsA[:, hq * 4:(hq + 1) * 4, :].rearrange("p h l -> p (h l)"), True, True,
                        [sc, s_rhsA], [pbs])
                eb = hq % 2
                self.act(Eb[eb], pb[:], AF.Exp, [pbs], [s_Eb[eb]])
                self.tt("dve", MT[:, hq * 4:(hq + 1) * 4, :], Eb[eb].rearrange("p (h l) -> p h l", h=4),
                        cbm[:, hq // 2, :].unsqueeze(1).to_broadcast([128, 4, 128]), ALU.mult, [s_Eb[eb], s_cbm], [s_MT[hq]])
            if lvl < 5:
                continue
            for g in range(8):
                pd, pds = self.bank()
                for h8 in range(8):
                    h = g * 8 + h8
                    self.mm(pd[:, h8 * 64:(h8 + 1) * 64], MT[:, h, :], xdt[:, h * 64:(h + 1) * 64], True, True,
                            [s_MT[h // 4], s_xdt], [pds])
                po, pos = self.bank()
                self.mm(po[:], bc[b][:, 8 + g, :], Rb[:, g * 512:(g + 1) * 512], True, True, [s_bc[b], s_Rb], [pos])
                yt = ytmp[g % 2]
                self.tt("dve", yt.rearrange("p (h q) -> p h q", h=8), po[:].rearrange("p (h q) -> p h q", h=8),
                        sm[:, 4, g * 8:(g + 1) * 8].unsqueeze(2).to_broadcast([128, 8, 64]), ALU.mult, [pos, s_sm[4]],
                        [s_ytmp[g % 2]])
                self.tt("dve", ych[0][:, g * 512:(g + 1) * 512], yt, pd[:], ALU.add, [s_ytmp[g % 2], pds], [s_y[0]])
            if lvl < 6:
                continue
            for g in range(8):
                pb, pbs = self.bank()
                self.mm(pb[:], xs[b][:, 4096 + g * 128:4096 + (g + 1) * 128], xdd[:, g * 512:(g + 1) * 512], True, True,
                        [s_xs[b], s_xdd], [pbs])
                Rg = R[:, g * 512:(g + 1) * 512]
                self.tt("dve", Rg.rearrange("p (h q) -> p h q", h=8), Rg.rearrange("p (h q) -> p h q", h=8),
                        sm[:, 5, g * 8:(g + 1) * 8].unsqueeze(2).to_broadcast([128, 8, 64]), ALU.mult, [s_R, s_sm[5]], [s_R])
                self.tt("dve", Rg, Rg, pb[:], ALU.add, [s_R, pbs], [s_R])
            self.cp("act", Rb, R, [s_R], [s_Rb])
            if lvl < 7:
                continue
            ob = i % 2
            if d == 0:
                self.cp("act", yo[ob], ych[0], [s_y[0]], [s_yo[ob]])
                P.dma("sp", S["yF"][t0:t0 + 128, :], yo[ob], reads=[s_yo[ob]], writes=[self.ds["yF"]])
            else:
                y = ych[0]
                P.dma("sp", yf[b], S["yF"][t0:t0 + 128, :], reads=[self.ds["yF"]], writes=[s_yf[b]])
                self.tt("dve", y, y, yf[b], ALU.add, [s_y[0], s_yf[b]], [s_y[0]])
                self.tt("pool", gtmp.rearrange("p (h q) -> p h q", h=64), x3,
                        self.dsk[:].unsqueeze(2).to_broadcast([128, 64, 64]), ALU.mult, [s_xs[b], self.s_par], [s_gtmp])
                self.tt("dve", y, y, gtmp, ALU.add, [s_y[0], s_gtmp], [s_y[0]])
                self.act(gtmp, zc[b], AF.Silu, [s_zc[b], s_gtmp], [s_gtmp])
                self.tt("dve", y, y, gtmp, ALU.mult, [s_y[0], s_gtmp], [s_y[0]])
                self.P.op("pool", lambda e: e.memset(gss, 0.0), [s_gss], [s_gss])
                for g in range(8):
                    self.P.op("act", lambda e, g=g, y=y: e.activation(out=sqj, in_=y[:, g * 512:(g + 1) * 512], func=AF.Square,
                                                                       accum_out=gss[:, g:g + 1]),
                              [s_y[0], s_sqj], [s_sqj, s_gss])
                self.act(gss, gss, AF.Sqrt, [s_gss, sc], [s_gss], bias=self.epsb[:], scale=1.0 / 512)
                self.P.op("dve", lambda e: e.reciprocal(out=gss, in_=gss), [s_gss], [s_gss])
                self.tt("dve", y.rearrange("p (g q) -> p g q", g=8), y.rearrange("p (g q) -> p g q", g=8),
                        gss.unsqueeze(2).to_broadcast([128, 8, 512]), ALU.mult, [s_y[0], s_gss], [s_y[0]])
                self.tt("dve", yo[ob], y, nw, ALU.mult, [s_y[0], s_nw], [s_yo[ob]])
                for q in range(8):
                    pb, pbs = self.bank()
                    pbb = pb[:].bitcast(BF16)
                    for k in range(4):
                        cc = q * 4 + k
                        self.tr(pbb[:, k * 128:(k + 1) * 128], yo[ob][:, cc * 128:(cc + 1) * 128], self.cmb[:, 0, :],
                                [s_yo[ob], sc], [pbs])
                    eng = "act" if q % 2 == 0 else "dve"
                    self.cp(eng, ost[ob][:, q * 4:(q + 1) * 4, :], pbb[:, 0:512].rearrange("p (k t) -> p k t", k=4), [pbs],
                            [s_ost[ob]])
                P.dma("sp", ssdT[:, :, t0:t0 + 128], ost[ob], reads=[s_ost[ob]], writes=[self.ds["ssdT"]])

    def phase_attn(self, l):
        P, A, S, I = self.P, self.A, self.S, self.I
        self.phase()
        sc = self.s_const
        bm = A.alloc(BF16, 3 * 8 * 512)
        s_bm = P.slot()
        P.dma("pool", bm[0:2, :], I["bmask"], writes=[s_bm])
        qT = [A.alloc(BF16, NTOK) for _ in range(2)]
        kT = [A.alloc(BF16, NTOK) for _ in range(2)]
        vh = [A.alloc(BF16, 34, 128) for _ in range(2)]
        tb = [A.alloc(BF16, 22, 64) for _ in range(2)]
        s_hd = P.slots(2)
        tmp = [A.alloc(F32, 512) for _ in range(2)]
        s_tmp = P.slots(2)
        PT = [A.alloc(BF16, 512) for _ in range(3)]
        s_PT = P.slots(3)
        rec = A.alloc(F32, 512)
        s_rec = P.slot()
        ost = [A.alloc(BF16, NTOK) for _ in range(2)]
        s_ost = P.slots(2)
        qk = S["qkT"]
        vv = S["vTM"].rearrange("(c p) n -> p c n", p=128)
        def loadh(h):
            b = h % 2
            P.dma("sp", qT[b], qk[h * 128:(h + 1) * 128, :], reads=[self.ds["qkT"]], writes=[s_hd[b]])
            P.dma("sp", kT[b], qk[D + h * 128:D + (h + 1) * 128, :], reads=[self.ds["qkT"]], writes=[s_hd[b]])
            P.dma("sp", vh[b], vv[:, :, h * 128:(h + 1) * 128], reads=[self.ds["vTM"]], writes=[s_hd[b]])
            P.dma("pool", tb[b].rearrange("p a b -> p (a b)"), I["rpbT"][l, h], writes=[s_hd[b]])
        loadh(0)
        nt = 0
        blk = 0
        for h in range(16):
            hb = h % 2
            if h + 1 < 16:
                loadh(h + 1)
            for Ib in range(-1, 8):
                if Ib < 0:
                    q0, qw = 0, 256
                    tiles = [(0, None, None), (1, None, None)]
                else:
                    q0, qw = NCTX + Ib * 512, 512
                    var = 0 if Ib == 0 else (2 if Ib == 7 else 1)
                    tiles = []
                    for j in range(8):
                        r = 8 * Ib - 4 + 2 * j
                        if 0 <= r <= 62:
                            tiles.append((2 + r // 2, j, var))
                    tiles += [(0, None, None), (1, None, None)]
                po, pos = self.ps[3 + blk % 2], self.pss[3 + blk % 2]
                pdn, pdns = self.ps[5 + blk % 2], self.pss[5 + blk % 2]
                blk += 1
                ntile = len(tiles)

                def qk_tile(ti):
                    kc, j, var = tiles[ti]
                    pi = (nt + ti) % 3
                    pS, pSs = self.ps[pi], self.pss[pi]
                    self.mm(pS[:, 0:qw], kT[hb][:, kc * 128:(kc + 1) * 128], qT[hb][:, q0:q0 + qw], True, j is None,
                            [s_hd[hb]], [pSs])
                    ptb = (nt + ti) % 3
                    if j is not None:
                        off = (var * 8 + j) * 512
                        self.mm(pS[:, 0:qw], self.a2[:], bm[0:2, off:off + 512], False, True, [sc, s_bm], [pSs])
                        tt_ = tmp[(nt + ti) % 2]
                        st_ = s_tmp[(nt + ti) % 2]
                        self.stt("dve", tt_.rearrange("p (a b) -> p a b", a=8), pS[:].rearrange("p (a b) -> p a b", a=8), SCALE,
                                 tb[hb][:, 14 - 2 * j:22 - 2 * j, :], ALU.mult, ALU.add, [pSs, s_hd[hb]], [st_])
                        self.act(PT[ptb], tt_, AF.Exp, [st_], [s_PT[ptb]])
                    else:
                        self.act(PT[ptb][:, 0:qw], pS[:, 0:qw], AF.Exp, [pSs], [s_PT[ptb]], scale=SCALE)

                def pv_tile(ti):
                    kc, j, var = tiles[ti]
                    ptb = (nt + ti) % 3
                    self.mm(po[:, 0:qw], vh[hb][:, kc, :], PT[ptb][:, 0:qw], ti == 0, ti == ntile - 1, [s_hd[hb], s_PT[ptb]], [pos])
                    self.mm(pdn[:, 0:qw], self.onesb[:], PT[ptb][:, 0:qw], ti == 0, ti == ntile - 1, [sc, s_PT[ptb]], [pdns])
                qk_tile(0)
                for ti in range(ntile):
                    if ti + 1 < ntile:
                        qk_tile(ti + 1)
                    pv_tile(ti)
                nt += ntile
                self.P.op("dve", lambda e, pdn=pdn, qw=qw: e.reciprocal(out=rec[:, 0:qw], in_=pdn[:, 0:qw]), [pdns], [s_rec])
                self.tt("dve", ost[hb][:, q0:q0 + qw], po[:, 0:qw], rec[:, 0:qw], ALU.mult, [pos, s_rec], [s_ost[hb]])
            P.dma("sp", S["attnT"][h * 128:(h + 1) * 128, :], ost[hb], reads=[s_ost[hb]], writes=[self.ds["attnT"]])

    def phase_merge(self, l):
        P, A, I, S = self.P, self.A, self.I, self.S
        self.wrr = 0
        for ps_, (src, KC, GM, W, gname) in enumerate((("attnT", 16, 2304, I["w_br_na"][l], 0),
                                                        ("ssdT", 32, 2176, I["w_br_ssd"][l], 1),
                                                        ("mT", 16, 2304, I["w_out"][l], None))):
            self.phase()
            G = GM
            groups = [(0, 2304), (2304, 2048)] if KC == 16 else [(0, 2176), (2176, 2176)]
            inT = A.alloc(BF16, KC, G)
            s_in = P.slot()
            sw = 512 if KC == 16 else 256
            wb = [A.alloc(BF16, KC, sw) for _ in range(2)]
            s_wb = P.slots(2)
            if ps_ < 2:
                gb = [A.alloc(BF16, G) for _ in range(2)]
                mb = [A.alloc(BF16, G) for _ in range(2)]
                s_gb = P.slots(2)
                s_mb = P.slots(2)
                sg = [A.alloc(F32, 512) for _ in range(2)]
                s_sg = P.slots(2)
                t2 = [A.alloc(F32, 512) for _ in range(2)]
                s_t2 = P.slots(2)
            else:
                xb = [A.alloc(F32, G) for _ in range(2)]
                s_xb = P.slots(2)
            for (g0, G) in groups:
                P.dma("sp", inT[:, :, 0:G], fm(S[src])[:, :, g0:g0 + G], reads=[self.ds[src]], writes=[s_in])
                ncx = NCTX - g0 if g0 < NCTX else 0

                def epi(pb, pbs, ch, t0, tw, g0=g0, G=G, ps_=ps_, gname=gname, ncx=ncx):
                    b = ch % 2
                    if ps_ < 2:
                        if t0 == 0:
                            r0 = gname * D + ch * 128
                            P.dma("sp", gb[b][:, 0:G], S["gT"][r0:r0 + 128, g0:g0 + G], reads=[self.ds["gT"]], writes=[s_gb[b]])
                            if ps_ == 1:
                                P.dma("sp", mb[b][:, 0:G], S["mT"][ch * 128:(ch + 1) * 128, g0:g0 + G], reads=[self.ds["mT"]],
                                      writes=[s_mb[b]])
                        k = (t0 // 512) % 2
                        self.act(sg[k][:, 0:tw], gb[b][:, t0:t0 + tw], AF.Sigmoid, [s_gb[b]], [s_sg[k]])
                        if ps_ == 0:
                            self.tt("dve", mb[b][:, t0:t0 + tw], pb[:, 0:tw], sg[k][:, 0:tw], ALU.mult, [pbs, s_sg[k]], [s_mb[b]])
                        else:
                            self.tt("dve", t2[k][:, 0:tw], pb[:, 0:tw], sg[k][:, 0:tw], ALU.mult, [pbs, s_sg[k]], [s_t2[k]])
                            self.tt("dve", mb[b][:, t0:t0 + tw], mb[b][:, t0:t0 + tw], t2[k][:, 0:tw], ALU.add,
                                    [s_mb[b], s_t2[k]], [s_mb[b]])
                        if t0 + tw == G:
                            P.dma("sp", S["mT"][ch * 128:(ch + 1) * 128, g0:g0 + G], mb[b][:, 0:G], reads=[s_mb[b]],
                                  writes=[self.ds["mT"]])
                    else:
                        self.resid_epi(pb, pbs, ch, t0, tw, g0, G, ncx, xb, s_xb, 2)
                self.gemm_fm(inT, s_in, KC, G, W, 0, D, sw, wb, s_wb, epi)

    def resid_epi(self, pb, pbs, ch, t0, tw, g0, G, ncx, xb, s_xb, pk):
        P, S = self.P, self.S
        b = ch % 2
        if t0 == 0:
            P.dma("sp", xb[b][:, 0:G], S["xT"][ch * 128:(ch + 1) * 128, g0:g0 + G], reads=[self.ds["xT"]], writes=[s_xb[b]])
        segs = []
        if ncx > t0:
            cw_ = min(ncx - t0, tw)
            segs.append((t0, cw_, 1))
            if cw_ < tw:
                segs.append((t0 + cw_, tw - cw_, 0))
        else:
            segs.append((t0, tw, 0))
        for (a, w, col) in segs:
            self.stt("dve", xb[b][:, a:a + w], pb[:, a - t0:a - t0 + w], self.par[:, pk, ch, col:col + 1], xb[b][:, a:a + w],
                     ALU.mult, ALU.add, [pbs, self.s_par, s_xb[b]], [s_xb[b]])
        if t0 + tw == G:
            P.dma("sp", S["xT"][ch * 128:(ch + 1) * 128, g0:g0 + G], xb[b][:, 0:G], reads=[s_xb[b]], writes=[self.ds["xT"]])

    def phase_ffn(self, l):
        P, A, I, S = self.P, self.A, self.I, self.S
        self.phase()
        self.wrr = 0
        GM = 2304
        hT = A.alloc(BF16, 16, GM)
        s_h = P.slot()
        wb = [A.alloc(BF16, 16, 512) for _ in range(2)]
        s_wb = P.slots(2)
        sgt = [A.alloc(F32, 512) for _ in range(2)]
        s_sgt = P.slots(2)
        stg = [A.alloc(BF16, GM) for _ in range(2)]
        s_stg = P.slots(2)
        W = I["w_gate_up"][l]
        Wv = fm(W)
        for (g0, G) in ((0, 2304), (2304, 2048)):
            self.norm_group(hT, s_h, g0, G, 3)
            for s in range(DFF // 256):
                b = self.wrr % 2
                self.wrr += 1
                P.dma("pool", wb[b][:, :, 0:256], Wv[:, :, s * 256:(s + 1) * 256], writes=[s_wb[b]])
                P.dma("pool", wb[b][:, :, 256:512], Wv[:, :, DFF + s * 256:DFF + (s + 1) * 256], writes=[s_wb[b]])
                for n in range(2):
                    ch = s * 2 + n
                    sb_ = ch % 2
                    for t0 in range(0, G, 512):
                        tw = min(512, G - t0)
                        pg, pgs = self.bank()
                        pu, pus = self.bank()
                        for kc in range(16):
                            self.mm(pg[:, 0:tw], wb[b][:, kc, n * 128:(n + 1) * 128], hT[:, kc, t0:t0 + tw], kc == 0, kc == 15,
                                    [s_wb[b], s_h], [pgs])
                        for kc in range(16):
                            self.mm(pu[:, 0:tw], wb[b][:, kc, 256 + n * 128:256 + (n + 1) * 128], hT[:, kc, t0:t0 + tw], kc == 0,
                                    kc == 15, [s_wb[b], s_h], [pus])
                        k = (t0 // 512) % 2
                        self.act(sgt[k][:, 0:tw], pg[:, 0:tw], AF.Silu, [pgs], [s_sgt[k]])
                        self.tt("dve", stg[sb_][:, t0:t0 + tw], pu[:, 0:tw], sgt[k][:, 0:tw], ALU.mult, [pus, s_sgt[k]], [s_stg[sb_]])
                    P.dma("sp", S["actT"][ch * 128:(ch + 1) * 128, g0:g0 + G], stg[sb_][:, 0:G], reads=[s_stg[sb_]],
                          writes=[self.ds["actT"]])
        self.phase()
        KC = DFF // 128
        G2 = 1536
        inT = A.alloc(BF16, KC, G2)
        s_in = P.slot()
        wb2 = [A.alloc(BF16, KC, 256) for _ in range(2)]
        s_wb2 = P.slots(2)
        xb = [A.alloc(F32, G2) for _ in range(2)]
        s_xb = P.slots(2)
        g0 = 0
        while g0 < NTOK:
            G = min(G2, NTOK - g0)
            P.dma("sp", inT[:, :, 0:G], fm(S["actT"])[:, :, g0:g0 + G], reads=[self.ds["actT"]], writes=[s_in])
            ncx = NCTX - g0 if g0 < NCTX else 0

            def epi(pb, pbs, ch, t0, tw, g0=g0, G=G, ncx=ncx):
                self.resid_epi(pb, pbs, ch, t0, tw, g0, G, ncx, xb, s_xb, 5)
            self.gemm_fm(inT, s_in, KC, G, I["w_down"][l], 0, D, 256, wb2, s_wb2, epi)
            g0 += G

    def on(self, name):
        ph = self.dbg.get("phases")
        return ph is None or name in ph

    def layer(self, l):
        self.phase()
        if self.on("params"):
            self.layer_params(l)
        if self.on("inproj"):
            self.phase_inproj(l)
        if self.on("conv"):
            self.phase_conv(l)
        if self.on("ssd0"):
            self.phase_ssd(l, 0)
        if self.on("ssd1"):
            self.phase_ssd(l, 1)
        if self.on("attn"):
            self.phase_attn(l)
        if self.on("merge"):
            self.phase_merge(l)
        if self.on("ffn"):
            self.phase_ffn(l)

    def final(self):
        P, A, S = self.P, self.A, self.S
        self.phase()
        sc = self.s_const
        xT = fm(S["xT"])
        xt = [A.alloc(F32, 16, 256) for _ in range(2)]
        s_xt = P.slots(2)
        sq = [A.alloc(BF16, 256) for _ in range(2)]
        s_sq = P.slots(2)
        rstd = A.alloc(F32, 256)
        s_r = P.slot()
        xn = [A.alloc(F32, 16, 256) for _ in range(1)]
        s_xn = P.slots(1)
        ot = [A.alloc(F32, D) for _ in range(2)]
        s_ot = P.slots(2)
        for ti in range(NLAT // 256):
            t0 = NCTX + ti * 256
            b = ti % 2
            P.dma("sp", xt[b], xT[:, :, t0:t0 + 256], reads=[self.ds["xT"]], writes=[s_xt[b]])
            pb, pbs = self.bank()
            for c in range(16):
                self.act(sq[c % 2], xt[b][:, c, :], AF.Square, [s_xt[b]], [s_sq[c % 2]])
                self.mm(pb[:, 0:256], self.meanb[:], sq[c % 2], c == 0, c == 15, [s_sq[c % 2], sc], [pbs])
            self.act(rstd, pb[:, 0:256], AF.Sqrt, [pbs, sc], [s_r], bias=self.epsb[:])
            self.P.op("dve", lambda e: e.reciprocal(out=rstd, in_=rstd), [s_r], [s_r])
            for c in range(16):
                self.stt("dve", xn[0][:, c, :], xt[b][:, c, :], self.nfin[:, c:c + 1], rstd, ALU.mult, ALU.mult,
                         [s_xt[b], sc, s_r], [s_xn[0]])
            for tc_ in range(2):
                ob = (ti * 2 + tc_) % 2
                for q in range(4):
                    pt, pts = self.bank()
                    for k in range(4):
                        c = q * 4 + k
                        self.tr(pt[:, k * 128:(k + 1) * 128], xn[0][:, c, tc_ * 128:(tc_ + 1) * 128], self.cm[:, 0, :],
                                [s_xn[0], sc], [pts])
                    eng = "act" if q % 2 == 0 else "dve"
                    self.cp(eng, ot[ob][:, q * 512:(q + 1) * 512], pt[:], [pts], [s_ot[ob]])
                r0 = ti * 256 + tc_ * 128
                P.dma("sp", self.out[r0:r0 + 128, :], ot[ob], reads=[s_ot[ob]], writes=[self.ds["out"]])


def _const_inputs():
    ident = np.eye(128, dtype=np.float32)
    t = np.arange(128)
    trif = (t[:, None] <= t[None, :]).astype(np.float32)
    trib = (t[:, None] >= t[None, :]).astype(np.float32)
    lstf = (t[:, None] > t[None, :]).astype(np.float32)
    lstb = (t[:, None] < t[None, :]).astype(np.float32)
    cmask = np.stack([ident, trif, trib, lstf, lstb])
    bm = np.zeros((2, 3, 8, 8, 64), np.float32)
    for v, Ib in enumerate((0, 3, 7)):
        for j in range(8):
            for kl in range(2):
                kr = 8 * Ib - 4 + 2 * j + kl
                for ql in range(8):
                    qr = 8 * Ib + ql
                    r0 = min(max(qr - 4, 0), 56)
                    ok = (0 <= kr <= 63) and (r0 <= kr < r0 + 8)
                    bm[kl, v, j, ql, :] = 0.0 if ok else NEG
    a2 = np.zeros((2, 128), np.float32)
    a2[0, :64] = 1.0
    a2[1, 64:] = 1.0
    return cmask, bm.reshape(2, -1), a2


def _rpb_tables(na_rpb):
    L = na_rpb.shape[0]
    kc = np.arange(64)[:, None]
    qc = np.arange(64)[None, :]
    col_start = np.clip(qc - 8, 0, 48)
    col_in = (kc >= col_start) & (kc < col_start + 16)
    dx = np.clip(kc - qc, -15, 15) + 15
    out = np.full((L, 16, 2, 64, 22, 64), NEG, np.float32)
    for kl in range(2):
        for sp in range(22):
            dy = 8 - (sp - 2) + kl
            if abs(dy) > 7:
                continue
            g = na_rpb[:, :, dy + 7, :][:, :, dx]
            out[:, :, kl, :, sp, :] = np.where(col_in[None, None], g, np.float32(NEG))
    return np.ascontiguousarray(out.reshape(L, 16, 128, 22 * 64))


def make_in_maps(inputs, ncores=4):
    cmask, bmask, a2 = _const_inputs()
    f = lambda a: np.ascontiguousarray(np.asarray(a, dtype=np.float32))
    shared = {
        "w_ada": f(inputs["w_ada"]),
        "b_ada": f(inputs["b_ada"]).reshape(DEPTH, 96, 128),
        "norms": np.ascontiguousarray(np.concatenate([f(inputs["norm_mix"]).reshape(DEPTH, 16, 128),
                                                      f(inputs["norm_ffn"]).reshape(DEPTH, 16, 128)], axis=1)),
        "norm_final": np.ascontiguousarray(np.tile(f(inputs["norm_final"]).reshape(16, 128), (2, 1))),
        "w_in": f(inputs["w_in"]),
        "rpbT": _rpb_tables(f(inputs["na_rpb"])),
        "conv_w": np.ascontiguousarray(np.concatenate([f(inputs["conv_w"]).reshape(DEPTH, 240, 128),
                                                       np.zeros((DEPTH, 16, 128), np.float32)], axis=1)),
        "conv_b": np.ascontiguousarray(np.concatenate([f(inputs["conv_b"]).reshape(DEPTH, 48, 128),
                                                       np.zeros((DEPTH, 16, 128), np.float32)], axis=1)),
        "dt_bias": f(inputs["dt_bias"]).reshape(DEPTH, 128),
        "a_log": f(inputs["a_log"]).reshape(DEPTH, 128),
        "d_skip": f(inputs["d_skip"]),
        "ssd_norm": f(inputs["ssd_norm"]),
        "w_br_na": f(inputs["w_br_na"]),
        "w_br_ssd": f(inputs["w_br_ssd"]),
        "w_out": f(inputs["w_out"]),
        "w_gate_up": f(inputs["w_gate_up"]),
        "w_down": f(inputs["w_down"]),
        "cmask": cmask, "bmask": bmask, "a2": a2,
    }
    x = f(inputs["x"])
    ctx = f(inputs["ctx"])
    c = f(inputs["c"])
    cc = f(inputs["c_ctx"])
    maps = []
    for b in range(ncores):
        m = dict(shared)
        m["x_tok"] = np.ascontiguousarray(np.concatenate([ctx[b], x[b]], axis=0))
        m["c2"] = np.ascontiguousarray(np.stack([c[b], cc]).reshape(32, 128))
        maps.append(m)
    return maps


def kernel(**inputs):
    nc = Builder().build()
    maps = make_in_maps(inputs, 4)
    res = run_bass_kernel_spmd(nc, maps, core_ids=list(range(4)))
    return np.stack([np.asarray(r["out"], dtype=np.float32) for r in res.results], axis=0)
```

```python
import contextlib
import numpy as np
import concourse.bass as bass
import concourse.mybir as mybir
from concourse.bass_utils import run_bass_kernel_spmd

F32 = mybir.dt.float32
BF16 = mybir.dt.bfloat16
AF = mybir.ActivationFunctionType
ALU = mybir.AluOpType

D = 2048
NCTX = 256
NLAT = 4096
NTOK = NCTX + NLAT
DEPTH = 4
NIN = 20608
DFF = 5632
EPS = 1e-6
NEG = -30000.0
SCALE = 128 ** -0.5
NCHAN = 8
SAME_ENGINE_SYNC = True


class Slot:
    __slots__ = ("name", "w", "r")

    def __init__(self, name):
        self.name = name
        self.w = None
        self.r = {}


class Op:
    __slots__ = ("eng", "fn", "waits", "stream", "seq")


class Prog:
    ENGS = ("pe", "act", "dve", "pool", "sp")

    def __init__(self, nc):
        self.nc = nc
        self.ops = {e: [] for e in self.ENGS}
        self.count = {}
        self.known = {e: {} for e in self.ENGS}
        self.needinc = {}
        self.chan_rr = {e: 0 for e in self.ENGS}
        self.nslots = 0

    def slot(self, name=None):
        self.nslots += 1
        return Slot(name or f"s{self.nslots}")

    def slots(self, n, name="s"):
        return [self.slot(f"{name}{i}") for i in range(n)]

    def _deps(self, eng, stream, reads, writes):
        deps = {}

        def add(d):
            if d is None:
                return
            st, sq = d
            if st == stream and "." in st:
                return
            if st == eng and (eng == "pe" or not SAME_ENGINE_SYNC):
                return
            if deps.get(st, 0) < sq:
                deps[st] = sq
        for s in reads:
            add(s.w)
        for s in writes:
            add(s.w)
            for st, sq in s.r.items():
                add((st, sq))
        out = []
        kn = self.known[eng]
        for st, sq in deps.items():
            if kn.get(st, 0) >= sq:
                continue
            kn[st] = sq
            out.append((st, sq))
            self.needinc.setdefault(st, set()).add(sq)
        return out

    def _commit(self, stream, seq, reads, writes):
        for s in reads:
            s.r[stream] = seq
        for s in writes:
            s.w = (stream, seq)
            s.r = {}

    def op(self, eng, fn, reads=(), writes=()):
        seq = self.count.get(eng, 0) + 1
        self.count[eng] = seq
        o = Op()
        o.eng = eng
        o.fn = fn
        o.stream = eng
        o.seq = seq
        o.waits = self._deps(eng, eng, reads, writes)
        self.ops[eng].append(o)
        self._commit(eng, seq, reads, writes)
        return o

    def dma(self, eng, out, in_, reads=(), writes=()):
        c = self.chan_rr[eng]
        self.chan_rr[eng] = (c + 1) % NCHAN
        stream = f"{eng}.ch{c}"
        seq = self.count.get(stream, 0) + 1
        self.count[stream] = seq
        o = Op()
        o.eng = eng
        o.fn = lambda e, out=out, in_=in_: e.dma_start(out=out, in_=in_)
        o.stream = stream
        o.seq = seq
        o.waits = self._deps(eng, stream, reads, writes)
        if seq > 1 and self.known[eng].get(stream, 0) < seq - 1:
            self.known[eng][stream] = seq - 1
            o.waits.append((stream, seq - 1))
        self.ops[eng].append(o)
        self._commit(stream, seq, reads, writes)
        return o

    def barrier(self):
        cur = dict(self.count)
        for eng in self.ENGS:
            o = Op()
            o.eng = eng
            o.fn = None
            o.stream = None
            o.seq = 0
            o.waits = []
            for st, sq in cur.items():
                if st == eng and eng == "pe":
                    continue
                if self.known[eng].get(st, 0) >= sq:
                    continue
                self.known[eng][st] = sq
                o.waits.append((st, sq))
                self.needinc.setdefault(st, set()).add(sq)
            self.ops[eng].append(o)

    def emit(self):
        nc = self.nc
        streams = set(self.count.keys())
        with contextlib.ExitStack() as es:
            sems = {}
            for st in sorted(streams):
                sems[st] = es.enter_context(nc.semaphore("sem_" + st.replace(".", "_")))
            val = {}
            for st in streams:
                if "." in st:
                    continue
                need = sorted(self.needinc.get(st, ()))
                val[st] = {sq: i + 1 for i, sq in enumerate(need)}
            block = es.enter_context(nc.Block())

            def run(eng):
                def body(e):
                    for o in self.ops[eng]:
                        for st, sq in o.waits:
                            if "." in st:
                                e.wait_ge(sems[st], 16 * sq)
                            else:
                                e.wait_ge(sems[st], val[st][sq])
                        if o.fn is None:
                            continue
                        ins = o.fn(e)
                        if "." in o.stream:
                            ins.then_inc(sems[o.stream], 16)
                        elif o.seq in val[o.stream]:
                            ins.then_inc(sems[o.stream], 1)
                return body
            block.tensor(run("pe"))
            block.scalar(run("act"))
            block.vector(run("dve"))
            block.gpsimd(run("pool"))
            block.sync(run("sp"))


class Arena:
    def __init__(self, handle, nelem):
        self.h = handle
        self.n = nelem
        self.off = 0

    def reset(self):
        self.off = 0

    def alloc(self, dt, *shape):
        n = 1
        for s in shape:
            n *= s
        ne = n * 2 if dt == F32 else n
        ne = (ne + 15) // 16 * 16
        assert self.off + ne <= self.n, f"arena overflow {self.off}+{ne}>{self.n}"
        v = self.h[:, self.off:self.off + ne]
        self.off += ne
        if dt == F32:
            v = v.bitcast(F32)
        v = v[:, 0:n]
        if len(shape) == 2:
            v = v.rearrange("p (a b) -> p a b", a=shape[0])
        elif len(shape) == 3:
            v = v.rearrange("p (a b c) -> p a b c", a=shape[0], b=shape[1])
        return v


def fm(ap):
    return ap.rearrange("(c p) t -> p c t", p=128)


class Builder:
    def __init__(self, nlayers=DEPTH, dbg=None):
        self.nlayers = nlayers
        self.dbg = dbg or {}
        self.nc = bass.Bass("TRN2", target_bir_lowering=False)
        self.P = Prog(self.nc)
        self.bank_rr = 0

    def din(self, name, shape, dt=F32):
        return self.nc.dram_tensor(name, list(shape), dt, kind="ExternalInput").ap()

    def dscr(self, name, shape, dt):
        kind = "ExternalOutput" if name in self.dbg else "Internal"
        return self.nc.dram_tensor(name, list(shape), dt, kind=kind).ap()

    def build(self):
        nc = self.nc
        P = self.P
        L = DEPTH
        I = {}
        tiny = self.dbg.get("tiny")
        I["x_tok"] = self.din("x_tok", [NTOK, D])
        I["c2"] = self.din("c2", [32, 128])
        I["w_ada"] = self.din("w_ada", [L, 128, 128] if tiny else [L, D, 6 * D])
        I["b_ada"] = self.din("b_ada", [L, 96, 128])
        I["norms"] = self.din("norms", [L, 32, 128])
        I["norm_final"] = self.din("norm_final", [32, 128])
        I["w_in"] = self.din("w_in", [L, 128, 128] if tiny else [L, D, NIN])
        I["rpbT"] = self.din("rpbT", [L, 16, 128, 22 * 64])
        I["conv_w"] = self.din("conv_w", [L, 256, 128])
        I["conv_b"] = self.din("conv_b", [L, 64, 128])
        I["dt_bias"] = self.din("dt_bias", [L, 128])
        I["a_log"] = self.din("a_log", [L, 128])
        I["d_skip"] = self.din("d_skip", [L, 64])
        I["ssd_norm"] = self.din("ssd_norm", [L, 4096])
        I["w_br_na"] = self.din("w_br_na", [L, 128, 128] if tiny else [L, D, D])
        I["w_br_ssd"] = self.din("w_br_ssd", [L, 128, 128] if tiny else [L, 2 * D, D])
        I["w_out"] = self.din("w_out", [L, 128, 128] if tiny else [L, D, D])
        I["w_gate_up"] = self.din("w_gate_up", [L, 128, 128] if tiny else [L, D, 2 * DFF])
        I["w_down"] = self.din("w_down", [L, 128, 128] if tiny else [L, DFF, D])
        I["cmask"] = self.din("cmask", [5, 128, 128])
        I["bmask"] = self.din("bmask", [2, 3 * 8 * 512])
        I["a2"] = self.din("a2", [2, 128])
        self.I = I
        self.out = nc.dram_tensor("out", [NLAT, D], F32, kind="ExternalOutput").ap()
        Sx = {}
        Sx["xT"] = self.dscr("xT", [D, NTOK], F32)
        Sx["qkT"] = self.dscr("qkT", [2 * D, NTOK], BF16)
        Sx["vTM"] = self.dscr("vTM", [NTOK, D], BF16)
        Sx["zTM"] = self.dscr("zTM", [NTOK, 2 * D], BF16)
        Sx["xbcT"] = self.dscr("xbcT", [6144, NTOK], BF16)
        Sx["dtTM"] = self.dscr("dtTM", [NTOK, 128], F32)
        Sx["gT"] = self.dscr("gT", [2 * D, NTOK], BF16)
        Sx["uBCT"] = self.dscr("uBCT", [2048, NTOK], BF16)
        Sx["xsB"] = self.dscr("xsB", [NTOK, 5120], BF16)
        Sx["yF"] = self.dscr("yF", [NTOK, 4096], BF16)
        Sx["ssdT"] = self.dscr("ssdT", [2 * D, NTOK], BF16)
        Sx["attnT"] = self.dscr("attnT", [D, NTOK], BF16)
        Sx["mT"] = self.dscr("mT", [D, NTOK], BF16)
        Sx["actT"] = self.dscr("actT", [DFF, NTOK], BF16)
        self.S = Sx
        self.ds = {k: P.slot("d_" + k) for k in Sx}
        self.ds["out"] = P.slot("d_out")

        with contextlib.ExitStack() as es:
            def sb(name, shape, dt=F32):
                return es.enter_context(nc.sbuf_tensor(name, list(shape), dt))
            self.ps = [es.enter_context(nc.psum_tensor(f"ps{i}", [128, 512], F32)) for i in range(8)]
            self.pss = P.slots(8, "ps")
            self.cm = sb("cm", [128, 5, 128])
            self.cmb = sb("cmb", [128, 5, 128], BF16)
            self.onesb = sb("onesb", [128, 128], BF16)
            self.meanb = sb("meanb", [128, 128], BF16)
            self.onesf = sb("onesf", [128, 128])
            self.epsb = sb("epsb", [128, 1])
            self.scT = sb("scT", [128, 2, 16])
            self.modT = sb("modT", [128, L, 96, 2])
            self.par = sb("par", [128, 6, 16, 2])
            self.nfin = sb("nfin", [128, 32])
            self.cw = sb("cw", [128, 256])
            self.cb = sb("cb", [128, 64])
            self.dtb = sb("dtb", [128, 128])
            self.aneg = sb("aneg", [128, 128])
            self.dsk = sb("dsk", [128, 64])
            self.a2 = sb("a2sb", [2, 128], BF16)
            ARENA = 98 * 1024
            self.arena_h = sb("arena", [128, ARENA], BF16)
            self.A = Arena(self.arena_h, ARENA)
            self.s_const = P.slot("const")
            self.s_par = P.slot("par")

            self.prologue()
            for l in range(self.nlayers):
                self.layer(l)
            if self.on("final"):
                self.final()
            P.barrier()
            P.emit()
        return nc

    def bank(self):
        i = self.bank_rr
        self.bank_rr = (i + 1) % 8
        return self.ps[i], self.pss[i]

    def phase(self):
        self.P.barrier()
        self.A.reset()

    def act(self, out, in_, func, reads, writes, **kw):
        self.P.op("act", lambda e: e.activation(out=out, in_=in_, func=func, **kw), reads, writes)

    def mm(self, out, lhsT, rhs, start, stop, reads, writes):
        self.P.op("pe", lambda e: e.matmul(out, lhsT, rhs, start=start, stop=stop), reads, writes)

    def tt(self, eng, out, in0, in1, op, reads, writes):
        self.P.op(eng, lambda e: e.tensor_tensor(out=out, in0=in0, in1=in1, op=op), reads, writes)

    def ts(self, eng, out, in0, s1, s2, op0, op1, reads, writes):
        self.P.op(eng, lambda e: e.tensor_scalar(out=out, in0=in0, scalar1=s1, scalar2=s2, op0=op0, op1=op1),
                  reads, writes)

    def stt(self, eng, out, in0, scalar, in1, op0, op1, reads, writes):
        self.P.op(eng, lambda e: e.scalar_tensor_tensor(out=out, in0=in0, scalar=scalar, in1=in1, op0=op0, op1=op1),
                  reads, writes)

    def cp(self, eng, out, in_, reads, writes):
        if eng == "act":
            self.act(out, in_, AF.Copy, reads, writes)
        else:
            self.P.op(eng, lambda e: e.tensor_copy(out=out, in_=in_), reads, writes)

    def tr(self, out, in_, ident, reads, writes):
        self.P.op("pe", lambda e: e.transpose(out, in_, ident), reads, writes)

    def small_T(self, src_dram, rows, dst, dst_slot):
        P = self.P
        A = self.A
        t = A.alloc(F32, 128)
        s = P.slot()
        P.dma("sp", t[0:rows, :], src_dram, writes=[s])
        pb, pbs = self.bank()
        self.tr(pb[:, 0:rows], t[0:rows, :], self.cm[0:rows, 0, 0:rows], [s, self.s_const], [pbs])
        self.cp("dve", dst, pb[:, 0:rows], [pbs], [dst_slot])

    def prologue(self):
        P, A, I = self.P, self.A, self.I
        sc = self.s_const
        P.dma("sp", self.cm[:], I["cmask"].rearrange("k p f -> p k f"), writes=[sc])
        P.op("dve", lambda e: e.tensor_copy(out=self.cmb[:], in_=self.cm[:]), [sc], [sc])
        P.op("pool", lambda e: e.memset(self.onesb[:], 1.0), [], [sc])
        P.op("pool", lambda e: e.memset(self.meanb[:], 1.0 / D), [], [sc])
        P.op("pool", lambda e: e.memset(self.onesf[:], 1.0), [], [sc])
        P.op("pool", lambda e: e.memset(self.epsb[:], EPS), [], [sc])
        P.dma("pool", self.a2[:], I["a2"], writes=[sc])
        self.small_T(I["norm_final"], 32, self.nfin[:], sc)
        c2raw = A.alloc(F32, 32)
        s_c2 = P.slot()
        self.small_T(I["c2"], 32, c2raw, s_c2)
        self.act(self.scT[:].rearrange("p r c -> p (r c)"), c2raw, AF.Silu, [s_c2], [sc])
        self.phase()
        if not self.on("prologue"):
            P.op("pool", lambda e: e.memset(self.modT[:], 0.1), [], [sc])
            return
        xT = fm(self.S["xT"])
        xin = [A.alloc(F32, D) for _ in range(2)]
        s_xin = P.slots(2)
        stg = [A.alloc(F32, 16, 512) for _ in range(1)]
        s_stg = P.slots(1)
        for c in range(NTOK // 128):
            b = c % 2
            P.dma("sp", xin[b], I["x_tok"][c * 128:(c + 1) * 128, :], writes=[s_xin[b]])
            for q in range(4):
                pb, pbs = self.bank()
                for k in range(4):
                    fc = q * 4 + k
                    self.tr(pb[:, k * 128:(k + 1) * 128], xin[b][:, fc * 128:(fc + 1) * 128], self.cm[:, 0, :],
                            [s_xin[b], sc], [pbs])
                eng = "act" if q % 2 == 0 else "dve"
                self.cp(eng, stg[0][:, q * 4:(q + 1) * 4, (c % 4) * 128:(c % 4 + 1) * 128],
                        pb[:].rearrange("p (k t) -> p k t", k=4), [pbs], [s_stg[0]])
            if c % 4 == 3 or c == NTOK // 128 - 1:
                t0 = (c // 4) * 512
                wv = (c % 4 + 1) * 128
                P.dma("sp", xT[:, :, t0:t0 + wv], stg[0][:, :, 0:wv], reads=[s_stg[0]], writes=[self.ds["xT"]])
        self.phase()
        wsl = [A.alloc(F32, 16, 512) for _ in range(2)]
        s_w = P.slots(2)
        badaT = A.alloc(F32, 96)
        s_b = P.slot()
        it = 0
        for l in range(self.nlayers):
            self.small_T(I["b_ada"][l], 96, badaT, s_b)
            pb, pbs = self.bank()
            for s in range(24):
                b = it % 2
                it += 1
                P.dma("sp", wsl[b], fm(I["w_ada"][l])[:, :, s * 512:(s + 1) * 512], writes=[s_w[b]])
                for n in range(4):
                    ch = s * 4 + n
                    for kc in range(16):
                        self.mm(pb[:, ch * 2:ch * 2 + 2], wsl[b][:, kc, n * 128:(n + 1) * 128], self.scT[:, :, kc],
                                kc == 0, kc == 15, [s_w[b], sc], [pbs])
            self.tt("dve", self.modT[:, l, :, :], pb[:, 0:192].rearrange("p (c r) -> p c r", r=2),
                    badaT.unsqueeze(2).to_broadcast([128, 96, 2]), ALU.add, [pbs, s_b], [sc])
        self.phase()

    def layer_params(self, l):
        P, A, I = self.P, self.A, self.I
        sp_ = self.s_par
        sc = self.s_const
        nmf = A.alloc(F32, 32)
        nm = nmf[:, 0:16]
        nf = nmf[:, 16:32]
        s_n = P.slot()
        self.small_T(I["norms"][l], 32, nmf, s_n)
        if "dbg_nmf" in self.dbg:
            dn = self.nc.dram_tensor("dbg_nmf", [128, 32], F32, kind="ExternalOutput").ap()
            P.dma("sp", dn, nmf, reads=[s_n], writes=[P.slot()])
        mod = self.modT
        for (k, nrm, sh, scl, g) in ((0, nm, 0, 1, 2), (3, nf, 3, 4, 5)):
            self.stt("dve", self.par[:, k, :, :], mod[:, l, scl * 16:(scl + 1) * 16, :], 1.0,
                     nrm.unsqueeze(2).to_broadcast([128, 16, 2]), ALU.add, ALU.mult, [sc, s_n], [sp_])
            self.cp("dve", self.par[:, k + 1, :, :], mod[:, l, sh * 16:(sh + 1) * 16, :], [sc], [sp_])
            self.cp("dve", self.par[:, k + 2, :, :], mod[:, l, g * 16:(g + 1) * 16, :], [sc], [sp_])
        self.small_T(I["conv_w"][l, 0:128, :], 128, self.cw[:, 0:128], sp_)
        self.small_T(I["conv_w"][l, 128:256, :], 128, self.cw[:, 128:256], sp_)
        self.small_T(I["conv_b"][l], 64, self.cb[:], sp_)
        P.dma("sp", self.dtb[:], I["dt_bias"][l].partition_broadcast(128), writes=[sp_])
        P.dma("sp", self.aneg[:], I["a_log"][l].partition_broadcast(128), writes=[sp_])
        P.dma("sp", self.dsk[:], I["d_skip"][l].partition_broadcast(128), writes=[sp_])
        self.act(self.aneg[:], self.aneg[:], AF.Exp, [sp_], [sp_])
        self.ts("dve", self.aneg[:], self.aneg[:], -1.0, 0.0, ALU.mult, ALU.add, [sp_], [sp_])

    def norm_group(self, hT, s_h, tok0, ntok, pk):
        P, A = self.P, self.A
        xT = fm(self.S["xT"])
        xt = [A.alloc(F32, 16, 256) for _ in range(2)]
        s_xt = P.slots(2)
        sq = [A.alloc(BF16, 256) for _ in range(2)]
        s_sq = P.slots(2)
        rstd = A.alloc(F32, 256)
        s_r = P.slot()
        tmp = [A.alloc(F32, 256) for _ in range(2)]
        s_tmp = P.slots(2)
        for ti in range(ntok // 256):
            t0 = tok0 + ti * 256
            col = 1 if t0 < NCTX else 0
            b = ti % 2
            P.dma("sp", xt[b], xT[:, :, t0:t0 + 256], reads=[self.ds["xT"]], writes=[s_xt[b]])
            pb, pbs = self.bank()
            for c in range(16):
                self.act(sq[c % 2], xt[b][:, c, :], AF.Square, [s_xt[b]], [s_sq[c % 2]])
                self.mm(pb[:, 0:256], self.meanb[:], sq[c % 2], c == 0, c == 15, [s_sq[c % 2], self.s_const], [pbs])
            self.act(rstd, pb[:, 0:256], AF.Sqrt, [pbs, self.s_const], [s_r], bias=self.epsb[:])
            self.P.op("dve", lambda e, rstd=rstd: e.reciprocal(out=rstd, in_=rstd), [s_r], [s_r])
            for c in range(16):
                self.stt("dve", tmp[c % 2], xt[b][:, c, :], self.par[:, pk, c, col:col + 1], rstd, ALU.mult, ALU.mult,
                         [s_xt[b], self.s_par, s_r], [s_tmp[c % 2]])
                self.act(hT[:, c, ti * 256:(ti + 1) * 256], tmp[c % 2], AF.Identity, [s_tmp[c % 2], self.s_par], [s_h],
                         bias=self.par[:, pk + 1, c, col:col + 1])

    def gemm_fm(self, inT, s_in, KC, ntok, W, col0, ncols, slab_w, wbufs, s_wb, epi, tile_w=512):
        P = self.P
        Wv = fm(W)
        nslab = (ncols + slab_w - 1) // slab_w
        for s in range(nslab):
            c0 = col0 + s * slab_w
            w = min(slab_w, col0 + ncols - c0)
            b = self.wrr % 2
            self.wrr += 1
            wb = wbufs[b]
            P.dma("pool", wb[:, 0:KC, 0:w], Wv[:, :, c0:c0 + w], writes=[s_wb[b]])
            for n in range(w // 128):
                ch = (c0 - col0) // 128 + n
                t0 = 0
                while t0 < ntok:
                    tw = min(tile_w, ntok - t0)
                    pb, pbs = self.bank()
                    for kc in range(KC):
                        self.mm(pb[:, 0:tw], wb[:, kc, n * 128:(n + 1) * 128], inT[:, kc, t0:t0 + tw], kc == 0, kc == KC - 1,
                                [s_wb[b], s_in], [pbs])
                    epi(pb, pbs, ch, t0, tw)
                    t0 += tw

    def gemm_tm(self, inT, s_in, KC, ntok, W, col0, ncols, slab_w, wbufs, s_wb, epi):
        P = self.P
        Wv = fm(W)
        nslab = (ncols + slab_w - 1) // slab_w
        for s in range(nslab):
            c0 = col0 + s * slab_w
            w = min(slab_w, col0 + ncols - c0)
            b = self.wrr % 2
            self.wrr += 1
            wb = wbufs[b]
            P.dma("pool", wb[:, 0:KC, 0:w], Wv[:, :, c0:c0 + w], writes=[s_wb[b]])
            for c in range(ntok // 128):
                pb, pbs = self.bank()
                for kc in range(KC):
                    self.mm(pb[:, 0:w], inT[:, kc, c * 128:(c + 1) * 128], wb[:, kc, 0:w], kc == 0, kc == KC - 1,
                            [s_wb[b], s_in], [pbs])
                epi(pb, pbs, c0 - col0, w, c)

    def phase_inproj(self, l):
        P, A, I, S = self.P, self.A, self.I, self.S
        self.phase()
        self.wrr = 0
        GM = 2304
        hT = A.alloc(BF16, 16, GM)
        s_h = P.slot()
        wb = [A.alloc(BF16, 16, 512) for _ in range(2)]
        s_wb = P.slots(2)
        stg = [A.alloc(BF16, GM) for _ in range(2)]
        s_stg = P.slots(2)
        stt_ = [A.alloc(BF16, 512) for _ in range(2)]
        s_stt = P.slots(2)
        stf = [A.alloc(F32, 128) for _ in range(2)]
        s_stf = P.slots(2)
        W = I["w_in"][l]
        self.ev = 0
        for (g0, G) in ((0, 2304), (2304, 2048)):
            self.norm_group(hT, s_h, g0, G, 0)
            if "dbg_hT" in self.dbg and g0 == 0:
                dh = self.nc.dram_tensor("dbg_hT", [128, 16, 2304], BF16, kind="ExternalOutput").ap()
                dm = self.nc.dram_tensor("dbg_mod", [128, DEPTH * 96 * 2], F32, kind="ExternalOutput").ap()
                dp = self.nc.dram_tensor("dbg_par", [128, 6 * 16 * 2], F32, kind="ExternalOutput").ap()
                sd = P.slot()
                P.dma("sp", dh, hT[:, :, 0:2304], reads=[s_h], writes=[sd])
                P.dma("sp", dm, self.modT[:].rearrange("p a b c -> p (a b c)"), reads=[self.s_const], writes=[sd])
                P.dma("sp", dp, self.par[:].rearrange("p a b c -> p (a b c)"), reads=[self.s_par], writes=[sd])

            def fm_epi(dst, dname, rowbase, g0=g0, G=G):
                def epi(pb, pbs, ch, t0, tw):
                    b = ch % 2
                    eng = "act" if self.ev % 2 == 0 else "dve"
                    self.ev += 1
                    self.cp(eng, stg[b][:, t0:t0 + tw], pb[:, 0:tw], [pbs], [s_stg[b]])
                    if t0 + tw == G:
                        r0 = rowbase + ch * 128
                        P.dma("sp", dst[r0:r0 + 128, g0:g0 + G], stg[b][:, 0:G], reads=[s_stg[b]], writes=[self.ds[dname]])
                return epi

            def tm_epi(dst, dname, colbase, g0=g0):
                def epi(pb, pbs, coff, w, c):
                    b = self.ev % 2
                    eng = "act" if self.ev % 2 == 0 else "dve"
                    self.ev += 1
                    self.cp(eng, stt_[b][:, 0:w], pb[:, 0:w], [pbs], [s_stt[b]])
                    t0 = g0 + c * 128
                    P.dma("sp", dst[t0:t0 + 128, colbase + coff:colbase + coff + w], stt_[b][:, 0:w], reads=[s_stt[b]],
                          writes=[self.ds[dname]])
                return epi

            def dt_epi(pb, pbs, coff, w, c, g0=g0):
                b = self.ev % 2
                self.ev += 1
                self.cp("dve", stf[b], pb[:, 0:128], [pbs], [s_stf[b]])
                t0 = g0 + c * 128
                P.dma("sp", S["dtTM"][t0:t0 + 128, :], stf[b], reads=[s_stf[b]], writes=[self.ds["dtTM"]])

            self.gemm_fm(hT, s_h, 16, G, W, 0, 4096, 512, wb, s_wb, fm_epi(S["qkT"], "qkT", 0))
            self.gemm_tm(hT, s_h, 16, G, W, 4096, 2048, 512, wb, s_wb, tm_epi(S["vTM"], "vTM", 0))
            self.gemm_tm(hT, s_h, 16, G, W, 6144, 4096, 512, wb, s_wb, tm_epi(S["zTM"], "zTM", 0))
            self.gemm_fm(hT, s_h, 16, G, W, 10240, 6144, 512, wb, s_wb, fm_epi(S["xbcT"], "xbcT", 0))
            self.gemm_tm(hT, s_h, 16, G, W, 16384, 128, 128, wb, s_wb, dt_epi)
            self.gemm_fm(hT, s_h, 16, G, W, 16512, 4096, 512, wb, s_wb, fm_epi(S["gT"], "gT", 0))

    def phase_conv(self, l):
        P, A, S = self.P, self.A, self.S
        self.phase()
        xbc = fm(S["xbcT"])
        uT = fm(S["uBCT"])
        NG = 8
        xin = [A.alloc(BF16, NG, 516) for _ in range(2)]
        s_xin = P.slots(2)
        acc = [A.alloc(F32, 512) for _ in range(4)]
        s_acc = P.slots(4)
        u = [A.alloc(BF16, NG, 512) for _ in range(2)]
        s_u = P.slots(2)
        xst = [A.alloc(BF16, 4, 1024) for _ in range(2)]
        s_xst = P.slots(2)
        tiles = [(0, 256)] + [(256 + 512 * i, 512) for i in range(8)]
        it = 0
        for (t0, w) in tiles:
            seq0 = 0 if t0 < NCTX else NCTX
            seq1 = NCTX if t0 < NCTX else NTOK
            for cg in range(48 // NG):
                b = it % 2
                it += 1
                lo = max(t0 - 2, seq0)
                hi = min(t0 + w + 2, seq1)
                if lo > t0 - 2:
                    P.op("pool", lambda e, b=b: e.memset(xin[b][:, :, 0:2], 0.0), [], [s_xin[b]])
                if hi < t0 + w + 2:
                    P.op("pool", lambda e, b=b, w=w: e.memset(xin[b][:, :, w + 2:w + 4], 0.0), [], [s_xin[b]])
                P.dma("sp", xin[b][:, :, lo - (t0 - 2):hi - (t0 - 2)], xbc[:, cg * NG:(cg + 1) * NG, lo:hi],
                      reads=[self.ds["xbcT"]], writes=[s_xin[b]])
                for k in range(NG):
                    cc = cg * NG + k
                    eng = "dve"
                    a = acc[cc % 4]
                    sa = s_acc[cc % 4]
                    self.ts(eng, a[:, 0:w], xin[b][:, k, 0:w], self.cw[:, cc:cc + 1], self.cb[:, cc:cc + 1], ALU.mult, ALU.add,
                            [s_xin[b], self.s_par], [sa])
                    for kk in range(1, 5):
                        self.stt(eng, a[:, 0:w], xin[b][:, k, kk:kk + w], self.cw[:, kk * 48 + cc:kk * 48 + cc + 1], a[:, 0:w],
                                 ALU.mult, ALU.add, [s_xin[b], self.s_par, sa], [sa])
                    self.act(u[b][:, k, 0:w], a[:, 0:w], AF.Silu, [sa], [s_u[b]])
                if cg >= 4:
                    P.dma("sp", uT[:, (cg - 4) * NG:(cg - 3) * NG, t0:t0 + w], u[b][:, :, 0:w], reads=[s_u[b]],
                          writes=[self.ds["uBCT"]])
                if cg < 5:
                    sb_ = it % 2
                    for tcn in range(w // 128):
                        for half in range(2):
                            pb, pbs = self.bank()
                            pbb = pb[:].bitcast(BF16)
                            for k in range(4):
                                kk = half * 4 + k
                                self.tr(pbb[:, k * 128:(k + 1) * 128], u[b][:, kk, tcn * 128:(tcn + 1) * 128], self.cmb[:, 0, :],
                                        [s_u[b], self.s_const], [pbs])
                            eng = "act" if half == 0 else "dve"
                            self.cp(eng, xst[sb_][:, tcn, half * 512:(half + 1) * 512], pbb[:, 0:512], [pbs], [s_xst[sb_]])
                    P.dma("sp", S["xsB"][t0:t0 + w, cg * 1024:(cg + 1) * 1024].rearrange("(c p) n -> p c n", p=128),
                          xst[sb_][:, 0:w // 128, :], reads=[s_xst[sb_]], writes=[self.ds["xsB"]])

    def phase_ssd(self, l, d):
        P, A, S, I = self.P, self.A, self.S, self.I
        self.phase()
        sc = self.s_const
        TRI = 1 + d
        LST = 3 + d
        order = list(range(34)) if d == 0 else [1, 0] + list(range(33, 1, -1))
        xs = [A.alloc(BF16, 5120) for _ in range(2)]
        s_xs = P.slots(2)
        bc = [A.alloc(BF16, 16, 128) for _ in range(2)]
        s_bc = P.slots(2)
        dtr = [A.alloc(F32, 128) for _ in range(2)]
        s_dtr = P.slots(2)
        R = A.alloc(F32, 4096)
        Rb = A.alloc(BF16, 4096)
        s_R = P.slot()
        s_Rb = P.slot()
        sm = A.alloc(F32, 8, 64)
        s_sm = P.slots(8)
        cst = A.alloc(F32, 128)
        s_cst = P.slot()
        xdt = A.alloc(BF16, 4096)
        xdd = A.alloc(BF16, 4096)
        s_xdt = P.slot()
        s_xdd = P.slot()
        cbm = A.alloc(BF16, 8, 128)
        s_cbm = P.slot()
        rhsA = A.alloc(BF16, 64, 128)
        s_rhsA = P.slots(2)
        Eb = [A.alloc(BF16, 512) for _ in range(2)]
        s_Eb = P.slots(2)
        MT = A.alloc(BF16, 64, 128)
        s_MT = P.slots(16)
        ych = [A.alloc(F32, 4096) for _ in range(1)]
        s_y = P.slots(1)
        ytmp = [A.alloc(F32, 512) for _ in range(2)]
        s_ytmp = P.slots(2)
        yo = [A.alloc(BF16, 4096) for _ in range(2 - d)] * (1 + d)
        s_yo = P.slots(2 - d) * (1 + d)
        if d == 1:
            yf = [A.alloc(BF16, 4096)] * 2
            s_yf = P.slots(1) * 2
            zc = [A.alloc(BF16, 4096) for _ in range(2)]
            s_zc = P.slots(2)
            nw = A.alloc(BF16, 4096)
            s_nw = P.slot()
            P.dma("pool", nw, I["ssd_norm"][l].partition_broadcast(128), writes=[s_nw])
            gss = A.alloc(F32, 8)
            s_gss = P.slot()
            gtmp = A.alloc(F32, 4096)
            s_gtmp = P.slot()
            sqj = A.alloc(BF16, 512)
            s_sqj = P.slot()
            ost = [A.alloc(BF16, 32, 128)] * 2
            s_ost = P.slots(1) * 2
            ssdT = fm(S["ssdT"])
        P.op("pool", lambda e: e.memset(R, 0.0), [], [s_R])
        P.op("pool", lambda e: e.memset(Rb, 0.0), [], [s_Rb])
        uT = fm(S["uBCT"])

        def load(i):
            c = order[i]
            b = i % 2
            t0 = c * 128
            P.dma("sp", xs[b], S["xsB"][t0:t0 + 128, :], reads=[self.ds["xsB"]], writes=[s_xs[b]])
            P.dma("sp", bc[b], uT[:, :, t0:t0 + 128], reads=[self.ds["uBCT"]], writes=[s_bc[b]])
            P.dma("sp", dtr[b], S["dtTM"][t0:t0 + 128, :], reads=[self.ds["dtTM"]], writes=[s_dtr[b]])
            if d == 1:
                P.dma("sp", zc[b], S["zTM"][t0:t0 + 128, :], reads=[self.ds["zTM"]], writes=[s_zc[b]])
        load(0)
        for i in range(34):
            c = order[i]
            b = i % 2
            t0 = c * 128
            if i + 1 < 34:
                load(i + 1)
            dsl = slice(d * 64, (d + 1) * 64)
            sub = self.dbg.get("ssd_sub", 99)
            if sub < 1:
                continue
            self.tt("dve", sm[:, 0, :], dtr[b][:, dsl], self.dtb[:, dsl], ALU.add, [s_dtr[b], self.s_par], [s_sm[0]])
            self.act(sm[:, 1, :], sm[:, 0, :], AF.Exp, [s_sm[0]], [s_sm[1]])
            self.act(sm[:, 2, :], sm[:, 1, :], AF.Ln, [s_sm[1], sc], [s_sm[2]], bias=self.onesf[:, 0:1])
            self.tt("dve", sm[:, 3, :], sm[:, 2, :], self.aneg[:, dsl], ALU.mult, [s_sm[2], self.s_par], [s_sm[3]])
            if sub < 2:
                continue
            pc, pcs = self.bank()
            self.mm(pc[:, 0:64], self.cm[:, TRI, :], sm[:, 3, :], True, True, [sc, s_sm[3]], [pcs])
            self.mm(pc[:, 64:128], self.onesf[:], sm[:, 3, :], True, True, [sc, s_sm[3]], [pcs])
            if sub < 3:
                continue
            self.cp("act", cst, pc[:, 0:128], [pcs], [s_cst])
            self.act(sm[:, 4, :], cst[:, 0:64], AF.Exp, [s_cst], [s_sm[4]])
            self.act(sm[:, 5, :], cst[:, 64:128], AF.Exp, [s_cst], [s_sm[5]])
            self.tt("dve", sm[:, 7, :], cst[:, 64:128], cst[:, 0:64], ALU.subtract, [s_cst], [s_sm[7]])
            self.act(sm[:, 6, :], sm[:, 7, :], AF.Exp, [s_sm[7]], [s_sm[6]])
            lvl = self.dbg.get("ssd_level", 99)
            if lvl < 1:
                continue
            x3 = xs[b][:, 0:4096].rearrange("p (h q) -> p h q", h=64)
            for hf in range(2):
                self.tt("pool", rhsA[:, hf * 32:(hf + 1) * 32, :],
                        sm[:, 3, hf * 32:(hf + 1) * 32].unsqueeze(2).to_broadcast([128, 32, 128]),
                        self.cm[:, TRI, :].unsqueeze(1).to_broadcast([128, 32, 128]), ALU.mult, [s_sm[3], sc], [s_rhsA[hf]])
            self.tt("pool", xdt.rearrange("p (h q) -> p h q", h=64), x3, sm[:, 2, :].unsqueeze(2).to_broadcast([128, 64, 64]),
                    ALU.mult, [s_xs[b], s_sm[2]], [s_xdt])
            self.tt("pool", xdd.rearrange("p (h q) -> p h q", h=64), xdt.rearrange("p (h q) -> p h q", h=64),
                    sm[:, 6, :].unsqueeze(2).to_broadcast([128, 64, 64]), ALU.mult, [s_xdt, s_sm[6]], [s_xdd])
            if lvl < 2:
                continue
            for half in range(2):
                pb, pbs = self.bank()
                for g4 in range(4):
                    g = half * 4 + g4
                    self.mm(pb[:, g4 * 128:(g4 + 1) * 128], bc[b][:, g, :], bc[b][:, 8 + g, :], True, True, [s_bc[b]], [pbs])
                self.tt("dve", cbm[:, half * 4:(half + 1) * 4, :], pb[:].rearrange("p (g l) -> p g l", g=4),
                        self.cm[:, TRI, :].unsqueeze(1).to_broadcast([128, 4, 128]), ALU.mult, [pbs, sc], [s_cbm])
            if lvl < 3:
                continue
            if lvl < 4:
                continue
            for hq in range(16):
                pb, pbs = self.bank()
                self.mm(pb[:], self.cmb[:, LST, :], rhsA[:, hq * 4:(hq + 1) * 4, :].rearrange("p h l -> p (h l)"), True, True,
                        [sc, s_rhsA[hq // 8]], [pbs])
                eb = hq % 2
                self.act(Eb[eb], pb[:], AF.Exp, [pbs], [s_Eb[eb]])
                self.tt("dve", MT[:, hq * 4:(hq + 1) * 4, :], Eb[eb].rearrange("p (h l) -> p h l", h=4),
                        cbm[:, hq // 2, :].unsqueeze(1).to_broadcast([128, 4, 128]), ALU.mult, [s_Eb[eb], s_cbm], [s_MT[hq]])
            if lvl < 5:
                continue
            for g in range(8):
                pd, pds = self.bank()
                for h8 in range(8):
                    h = g * 8 + h8
                    self.mm(pd[:, h8 * 64:(h8 + 1) * 64], MT[:, h, :], xdt[:, h * 64:(h + 1) * 64], True, True,
                            [s_MT[h // 4], s_xdt], [pds])
                po, pos = self.bank()
                self.mm(po[:], bc[b][:, 8 + g, :], Rb[:, g * 512:(g + 1) * 512], True, True, [s_bc[b], s_Rb], [pos])
                yt = ytmp[g % 2]
                self.tt("dve", yt.rearrange("p (h q) -> p h q", h=8), po[:].rearrange("p (h q) -> p h q", h=8),
                        sm[:, 4, g * 8:(g + 1) * 8].unsqueeze(2).to_broadcast([128, 8, 64]), ALU.mult, [pos, s_sm[4]],
                        [s_ytmp[g % 2]])
                self.tt("dve", ych[0][:, g * 512:(g + 1) * 512], yt, pd[:], ALU.add, [s_ytmp[g % 2], pds], [s_y[0]])
            if lvl < 6:
                continue
            for g in range(8):
                pb, pbs = self.bank()
                self.mm(pb[:], xs[b][:, 4096 + g * 128:4096 + (g + 1) * 128], xdd[:, g * 512:(g + 1) * 512], True, True,
                        [s_xs[b], s_xdd], [pbs])
                Rg = R[:, g * 512:(g + 1) * 512]
                self.tt("dve", Rg.rearrange("p (h q) -> p h q", h=8), Rg.rearrange("p (h q) -> p h q", h=8),
                        sm[:, 5, g * 8:(g + 1) * 8].unsqueeze(2).to_broadcast([128, 8, 64]), ALU.mult, [s_R, s_sm[5]], [s_R])
                self.tt("dve", Rg, Rg, pb[:], ALU.add, [s_R, pbs], [s_R])
            self.cp("act", Rb, R, [s_R], [s_Rb])
            if lvl < 7:
                continue
            ob = i % 2
            if d == 0:
                self.cp("act", yo[ob], ych[0], [s_y[0]], [s_yo[ob]])
                P.dma("sp", S["yF"][t0:t0 + 128, :], yo[ob], reads=[s_yo[ob]], writes=[self.ds["yF"]])
            else:
                y = ych[0]
                P.dma("sp", yf[b], S["yF"][t0:t0 + 128, :], reads=[self.ds["yF"]], writes=[s_yf[b]])
                self.tt("dve", y, y, yf[b], ALU.add, [s_y[0], s_yf[b]], [s_y[0]])
                self.tt("pool", gtmp.rearrange("p (h q) -> p h q", h=64), x3,
                        self.dsk[:].unsqueeze(2).to_broadcast([128, 64, 64]), ALU.mult, [s_xs[b], self.s_par], [s_gtmp])
                self.tt("dve", y, y, gtmp, ALU.add, [s_y[0], s_gtmp], [s_y[0]])
                self.act(gtmp, zc[b], AF.Silu, [s_zc[b], s_gtmp], [s_gtmp])
                self.tt("dve", y, y, gtmp, ALU.mult, [s_y[0], s_gtmp], [s_y[0]])
                self.P.op("pool", lambda e: e.memset(gss, 0.0), [s_gss], [s_gss])
                for g in range(8):
                    self.P.op("act", lambda e, g=g, y=y: e.activation(out=sqj, in_=y[:, g * 512:(g + 1) * 512], func=AF.Square,
                                                                       accum_out=gss[:, g:g + 1]),
                              [s_y[0], s_sqj], [s_sqj, s_gss])
                self.act(gss, gss, AF.Sqrt, [s_gss, sc], [s_gss], bias=self.epsb[:], scale=1.0 / 512)
                self.P.op("dve", lambda e: e.reciprocal(out=gss, in_=gss), [s_gss], [s_gss])
                self.tt("dve", y.rearrange("p (g q) -> p g q", g=8), y.rearrange("p (g q) -> p g q", g=8),
                        gss.unsqueeze(2).to_broadcast([128, 8, 512]), ALU.mult, [s_y[0], s_gss], [s_y[0]])
                self.tt("dve", yo[ob], y, nw, ALU.mult, [s_y[0], s_nw], [s_yo[ob]])
                for q in range(8):
                    pb, pbs = self.bank()
                    pbb = pb[:].bitcast(BF16)
                    for k in range(4):
                        cc = q * 4 + k
                        self.tr(pbb[:, k * 128:(k + 1) * 128], yo[ob][:, cc * 128:(cc + 1) * 128], self.cmb[:, 0, :],
                                [s_yo[ob], sc], [pbs])
                    eng = "act" if q % 2 == 0 else "dve"
                    self.cp(eng, ost[ob][:, q * 4:(q + 1) * 4, :], pbb[:, 0:512].rearrange("p (k t) -> p k t", k=4), [pbs],
                            [s_ost[ob]])
                P.dma("sp", ssdT[:, :, t0:t0 + 128], ost[ob], reads=[s_ost[ob]], writes=[self.ds["ssdT"]])

    def phase_attn(self, l):
        P, A, S, I = self.P, self.A, self.S, self.I
        self.phase()
        sc = self.s_const
        bm = A.alloc(BF16, 3 * 8 * 512)
        s_bm = P.slot()
        P.dma("pool", bm[0:2, :], I["bmask"], writes=[s_bm])
        qT = [A.alloc(BF16, NTOK) for _ in range(2)]
        kT = [A.alloc(BF16, NTOK) for _ in range(2)]
        vh = [A.alloc(BF16, 34, 128) for _ in range(2)]
        tb = [A.alloc(BF16, 22, 64) for _ in range(2)]
        s_hd = P.slots(2)
        tmp = [A.alloc(F32, 512) for _ in range(2)]
        s_tmp = P.slots(2)
        PT = [A.alloc(BF16, 512) for _ in range(3)]
        s_PT = P.slots(3)
        rec = A.alloc(F32, 512)
        s_rec = P.slot()
        ost = [A.alloc(BF16, NTOK) for _ in range(2)]
        s_ost = P.slots(2)
        qk = S["qkT"]
        vv = S["vTM"].rearrange("(c p) n -> p c n", p=128)
        def loadh(h):
            b = h % 2
            P.dma("sp", qT[b], qk[h * 128:(h + 1) * 128, :], reads=[self.ds["qkT"]], writes=[s_hd[b]])
            P.dma("sp", kT[b], qk[D + h * 128:D + (h + 1) * 128, :], reads=[self.ds["qkT"]], writes=[s_hd[b]])
            P.dma("sp", vh[b], vv[:, :, h * 128:(h + 1) * 128], reads=[self.ds["vTM"]], writes=[s_hd[b]])
            P.dma("pool", tb[b].rearrange("p a b -> p (a b)"), I["rpbT"][l, h], writes=[s_hd[b]])
        loadh(0)
        nt = 0
        blk = 0
        for h in range(16):
            hb = h % 2
            if h + 1 < 16:
                loadh(h + 1)
            for Ib in range(-1, 8):
                if Ib < 0:
                    q0, qw = 0, 256
                    tiles = [(0, None, None), (1, None, None)]
                else:
                    q0, qw = NCTX + Ib * 512, 512
                    var = 0 if Ib == 0 else (2 if Ib == 7 else 1)
                    tiles = []
                    for j in range(8):
                        r = 8 * Ib - 4 + 2 * j
                        if 0 <= r <= 62:
                            tiles.append((2 + r // 2, j, var))
                    tiles += [(0, None, None), (1, None, None)]
                po, pos = self.ps[3 + blk % 2], self.pss[3 + blk % 2]
                pdn, pdns = self.ps[5 + blk % 2], self.pss[5 + blk % 2]
                blk += 1
                ntile = len(tiles)

                def qk_tile(ti):
                    kc, j, var = tiles[ti]
                    pi = (nt + ti) % 3
                    pS, pSs = self.ps[pi], self.pss[pi]
                    self.mm(pS[:, 0:qw], kT[hb][:, kc * 128:(kc + 1) * 128], qT[hb][:, q0:q0 + qw], True, j is None,
                            [s_hd[hb]], [pSs])
                    ptb = (nt + ti) % 3
                    if j is not None:
                        off = (var * 8 + j) * 512
                        self.mm(pS[:, 0:qw], self.a2[:], bm[0:2, off:off + 512], False, True, [sc, s_bm], [pSs])
                        tt_ = tmp[(nt + ti) % 2]
                        st_ = s_tmp[(nt + ti) % 2]
                        self.stt("dve", tt_.rearrange("p (a b) -> p a b", a=8), pS[:].rearrange("p (a b) -> p a b", a=8), SCALE,
                                 tb[hb][:, 14 - 2 * j:22 - 2 * j, :], ALU.mult, ALU.add, [pSs, s_hd[hb]], [st_])
                        self.act(PT[ptb], tt_, AF.Exp, [st_], [s_PT[ptb]])
                    else:
                        self.act(PT[ptb][:, 0:qw], pS[:, 0:qw], AF.Exp, [pSs], [s_PT[ptb]], scale=SCALE)

                def pv_tile(ti):
                    kc, j, var = tiles[ti]
                    ptb = (nt + ti) % 3
                    self.mm(po[:, 0:qw], vh[hb][:, kc, :], PT[ptb][:, 0:qw], ti == 0, ti == ntile - 1, [s_hd[hb], s_PT[ptb]], [pos])
                    self.mm(pdn[:, 0:qw], self.onesb[:], PT[ptb][:, 0:qw], ti == 0, ti == ntile - 1, [sc, s_PT[ptb]], [pdns])
                qk_tile(0)
                if ntile > 1:
                    qk_tile(1)
                for ti in range(ntile):
                    if ti + 2 < ntile:
                        qk_tile(ti + 2)
                    pv_tile(ti)
                nt += ntile
                self.P.op("dve", lambda e, pdn=pdn, qw=qw: e.reciprocal(out=rec[:, 0:qw], in_=pdn[:, 0:qw]), [pdns], [s_rec])
                self.tt("dve", ost[hb][:, q0:q0 + qw], po[:, 0:qw], rec[:, 0:qw], ALU.mult, [pos, s_rec], [s_ost[hb]])
            P.dma("sp", S["attnT"][h * 128:(h + 1) * 128, :], ost[hb], reads=[s_ost[hb]], writes=[self.ds["attnT"]])

    def phase_merge(self, l):
        P, A, I, S = self.P, self.A, self.I, self.S
        self.wrr = 0
        for ps_, (src, KC, GM, W, gname) in enumerate((("attnT", 16, 2304, I["w_br_na"][l], 0),
                                                        ("ssdT", 32, 1088, I["w_br_ssd"][l], 1),
                                                        ("mT", 16, 2304, I["w_out"][l], None))):
            self.phase()
            G = GM
            groups = [(0, 2304), (2304, 2048)] if KC == 16 else [(i * 1088, 1088) for i in range(4)]
            inT = A.alloc(BF16, KC, G)
            s_in = P.slot()
            sw = 512 if KC == 16 else 256
            wb = [A.alloc(BF16, KC, sw) for _ in range(2)]
            s_wb = P.slots(2)
            if ps_ < 2:
                gb = [A.alloc(BF16, G) for _ in range(2)]
                mb = [A.alloc(BF16, G) for _ in range(2)]
                s_gb = P.slots(2)
                s_mb = P.slots(2)
                sg = [A.alloc(F32, 512) for _ in range(2)]
                s_sg = P.slots(2)
                t2 = [A.alloc(F32, 512) for _ in range(2)]
                s_t2 = P.slots(2)
            else:
                xb = [A.alloc(F32, G) for _ in range(2)]
                s_xb = P.slots(2)
            for (g0, G) in groups:
                P.dma("sp", inT[:, :, 0:G], fm(S[src])[:, :, g0:g0 + G], reads=[self.ds[src]], writes=[s_in])
                ncx = NCTX - g0 if g0 < NCTX else 0

                def epi(pb, pbs, ch, t0, tw, g0=g0, G=G, ps_=ps_, gname=gname, ncx=ncx):
                    b = ch % 2
                    if ps_ < 2:
                        if t0 == 0:
                            r0 = gname * D + ch * 128
                            P.dma("sp", gb[b][:, 0:G], S["gT"][r0:r0 + 128, g0:g0 + G], reads=[self.ds["gT"]], writes=[s_gb[b]])
                            if ps_ == 1:
                                P.dma("sp", mb[b][:, 0:G], S["mT"][ch * 128:(ch + 1) * 128, g0:g0 + G], reads=[self.ds["mT"]],
                                      writes=[s_mb[b]])
                        k = (t0 // 512) % 2
                        self.act(sg[k][:, 0:tw], gb[b][:, t0:t0 + tw], AF.Sigmoid, [s_gb[b]], [s_sg[k]])
                        if ps_ == 0:
                            self.tt("dve", mb[b][:, t0:t0 + tw], pb[:, 0:tw], sg[k][:, 0:tw], ALU.mult, [pbs, s_sg[k]], [s_mb[b]])
                        else:
                            self.tt("dve", t2[k][:, 0:tw], pb[:, 0:tw], sg[k][:, 0:tw], ALU.mult, [pbs, s_sg[k]], [s_t2[k]])
                            self.tt("dve", mb[b][:, t0:t0 + tw], mb[b][:, t0:t0 + tw], t2[k][:, 0:tw], ALU.add,
                                    [s_mb[b], s_t2[k]], [s_mb[b]])
                        if t0 + tw == G:
                            P.dma("sp", S["mT"][ch * 128:(ch + 1) * 128, g0:g0 + G], mb[b][:, 0:G], reads=[s_mb[b]],
                                  writes=[self.ds["mT"]])
                    else:
                        self.resid_epi(pb, pbs, ch, t0, tw, g0, G, ncx, xb, s_xb, 2)
                self.gemm_fm(inT, s_in, KC, G, W, 0, D, sw, wb, s_wb, epi)

    def resid_epi(self, pb, pbs, ch, t0, tw, g0, G, ncx, xb, s_xb, pk):
        P, S = self.P, self.S
        b = ch % 2
        if t0 == 0:
            P.dma("sp", xb[b][:, 0:G], S["xT"][ch * 128:(ch + 1) * 128, g0:g0 + G], reads=[self.ds["xT"]], writes=[s_xb[b]])
        segs = []
        if ncx > t0:
            cw_ = min(ncx - t0, tw)
            segs.append((t0, cw_, 1))
            if cw_ < tw:
                segs.append((t0 + cw_, tw - cw_, 0))
        else:
            segs.append((t0, tw, 0))
        for (a, w, col) in segs:
            self.stt("dve", xb[b][:, a:a + w], pb[:, a - t0:a - t0 + w], self.par[:, pk, ch, col:col + 1], xb[b][:, a:a + w],
                     ALU.mult, ALU.add, [pbs, self.s_par, s_xb[b]], [s_xb[b]])
        if t0 + tw == G:
            P.dma("sp", S["xT"][ch * 128:(ch + 1) * 128, g0:g0 + G], xb[b][:, 0:G], reads=[s_xb[b]], writes=[self.ds["xT"]])

    def phase_ffn(self, l):
        P, A, I, S = self.P, self.A, self.I, self.S
        self.phase()
        self.wrr = 0
        GM = 2304
        hT = A.alloc(BF16, 16, GM)
        s_h = P.slot()
        wb = [A.alloc(BF16, 16, 512) for _ in range(2)]
        s_wb = P.slots(2)
        sgt = [A.alloc(F32, 512) for _ in range(2)]
        s_sgt = P.slots(2)
        stg = [A.alloc(BF16, GM) for _ in range(2)]
        s_stg = P.slots(2)
        W = I["w_gate_up"][l]
        Wv = fm(W)
        for (g0, G) in ((0, 2304), (2304, 2048)):
            self.norm_group(hT, s_h, g0, G, 3)
            for s in range(DFF // 256):
                b = self.wrr % 2
                self.wrr += 1
                P.dma("pool", wb[b][:, :, 0:256], Wv[:, :, s * 256:(s + 1) * 256], writes=[s_wb[b]])
                P.dma("pool", wb[b][:, :, 256:512], Wv[:, :, DFF + s * 256:DFF + (s + 1) * 256], writes=[s_wb[b]])
                for n in range(2):
                    ch = s * 2 + n
                    sb_ = ch % 2
                    for t0 in range(0, G, 512):
                        tw = min(512, G - t0)
                        pg, pgs = self.bank()
                        pu, pus = self.bank()
                        for kc in range(16):
                            self.mm(pg[:, 0:tw], wb[b][:, kc, n * 128:(n + 1) * 128], hT[:, kc, t0:t0 + tw], kc == 0, kc == 15,
                                    [s_wb[b], s_h], [pgs])
                        for kc in range(16):
                            self.mm(pu[:, 0:tw], wb[b][:, kc, 256 + n * 128:256 + (n + 1) * 128], hT[:, kc, t0:t0 + tw], kc == 0,
                                    kc == 15, [s_wb[b], s_h], [pus])
                        k = (t0 // 512) % 2
                        self.act(sgt[k][:, 0:tw], pg[:, 0:tw], AF.Silu, [pgs], [s_sgt[k]])
                        self.tt("dve", stg[sb_][:, t0:t0 + tw], pu[:, 0:tw], sgt[k][:, 0:tw], ALU.mult, [pus, s_sgt[k]], [s_stg[sb_]])
                    P.dma("sp", S["actT"][ch * 128:(ch + 1) * 128, g0:g0 + G], stg[sb_][:, 0:G], reads=[s_stg[sb_]],
                          writes=[self.ds["actT"]])
        self.phase()
        KC = DFF // 128
        G2 = 768
        inT = A.alloc(BF16, KC, G2)
        s_in = P.slot()
        wb2 = [A.alloc(BF16, KC, 256) for _ in range(2)]
        s_wb2 = P.slots(2)
        xb = [A.alloc(F32, G2) for _ in range(2)]
        s_xb = P.slots(2)
        g0 = 0
        while g0 < NTOK:
            G = min(G2, NTOK - g0)
            P.dma("sp", inT[:, :, 0:G], fm(S["actT"])[:, :, g0:g0 + G], reads=[self.ds["actT"]], writes=[s_in])
            ncx = NCTX - g0 if g0 < NCTX else 0

            def epi(pb, pbs, ch, t0, tw, g0=g0, G=G, ncx=ncx):
                self.resid_epi(pb, pbs, ch, t0, tw, g0, G, ncx, xb, s_xb, 5)
            self.gemm_fm(inT, s_in, KC, G, I["w_down"][l], 0, D, 256, wb2, s_wb2, epi)
            g0 += G

    def on(self, name):
        ph = self.dbg.get("phases")
        return ph is None or name in ph

    def layer(self, l):
        self.phase()
        if self.on("params"):
            self.layer_params(l)
        if self.on("inproj"):
            self.phase_inproj(l)
        if self.on("conv"):
            self.phase_conv(l)
        if self.on("ssd0"):
            self.phase_ssd(l, 0)
        if self.on("ssd1"):
            self.phase_ssd(l, 1)
        if self.on("attn"):
            self.phase_attn(l)
        if self.on("merge"):
            self.phase_merge(l)
        if self.on("ffn"):
            self.phase_ffn(l)

    def final(self):
        P, A, S = self.P, self.A, self.S
        self.phase()
        sc = self.s_const
        xT = fm(S["xT"])
        xt = [A.alloc(F32, 16, 256) for _ in range(2)]
        s_xt = P.slots(2)
        sq = [A.alloc(BF16, 256) for _ in range(2)]
        s_sq = P.slots(2)
        rstd = A.alloc(F32, 256)
        s_r = P.slot()
        xn = [A.alloc(F32, 16, 256) for _ in range(1)]
        s_xn = P.slots(1)
        ot = [A.alloc(F32, D) for _ in range(2)]
        s_ot = P.slots(2)
        for ti in range(NLAT // 256):
            t0 = NCTX + ti * 256
            b = ti % 2
            P.dma("sp", xt[b], xT[:, :, t0:t0 + 256], reads=[self.ds["xT"]], writes=[s_xt[b]])
            pb, pbs = self.bank()
            for c in range(16):
                self.act(sq[c % 2], xt[b][:, c, :], AF.Square, [s_xt[b]], [s_sq[c % 2]])
                self.mm(pb[:, 0:256], self.meanb[:], sq[c % 2], c == 0, c == 15, [s_sq[c % 2], sc], [pbs])
            self.act(rstd, pb[:, 0:256], AF.Sqrt, [pbs, sc], [s_r], bias=self.epsb[:])
            self.P.op("dve", lambda e: e.reciprocal(out=rstd, in_=rstd), [s_r], [s_r])
            for c in range(16):
                self.stt("dve", xn[0][:, c, :], xt[b][:, c, :], self.nfin[:, c:c + 1], rstd, ALU.mult, ALU.mult,
                         [s_xt[b], sc, s_r], [s_xn[0]])
            for tc_ in range(2):
                ob = (ti * 2 + tc_) % 2
                for q in range(4):
                    pt, pts = self.bank()
                    for k in range(4):
                        c = q * 4 + k
                        self.tr(pt[:, k * 128:(k + 1) * 128], xn[0][:, c, tc_ * 128:(tc_ + 1) * 128], self.cm[:, 0, :],
                                [s_xn[0], sc], [pts])
                    eng = "act" if q % 2 == 0 else "dve"
                    self.cp(eng, ot[ob][:, q * 512:(q + 1) * 512], pt[:], [pts], [s_ot[ob]])
                r0 = ti * 256 + tc_ * 128
                P.dma("sp", self.out[r0:r0 + 128, :], ot[ob], reads=[s_ot[ob]], writes=[self.ds["out"]])


def _const_inputs():
    ident = np.eye(128, dtype=np.float32)
    t = np.arange(128)
    trif = (t[:, None] <= t[None, :]).astype(np.float32)
    trib = (t[:, None] >= t[None, :]).astype(np.float32)
    lstf = (t[:, None] > t[None, :]).astype(np.float32)
    lstb = (t[:, None] < t[None, :]).astype(np.float32)
    cmask = np.stack([ident, trif, trib, lstf, lstb])
    bm = np.zeros((2, 3, 8, 8, 64), np.float32)
    for v, Ib in enumerate((0, 3, 7)):
        for j in range(8):
            for kl in range(2):
                kr = 8 * Ib - 4 + 2 * j + kl
                for ql in range(8):
                    qr = 8 * Ib + ql
                    r0 = min(max(qr - 4, 0), 56)
                    ok = (0 <= kr <= 63) and (r0 <= kr < r0 + 8)
                    bm[kl, v, j, ql, :] = 0.0 if ok else NEG
    a2 = np.zeros((2, 128), np.float32)
    a2[0, :64] = 1.0
    a2[1, 64:] = 1.0
    return cmask, bm.reshape(2, -1), a2


def _rpb_tables(na_rpb):
    L = na_rpb.shape[0]
    kc = np.arange(64)[:, None]
    qc = np.arange(64)[None, :]
    col_start = np.clip(qc - 8, 0, 48)
    col_in = (kc >= col_start) & (kc < col_start + 16)
    dx = np.clip(kc - qc, -15, 15) + 15
    out = np.full((L, 16, 2, 64, 22, 64), NEG, np.float32)
    for kl in range(2):
        for sp in range(22):
            dy = 8 - (sp - 2) + kl
            if abs(dy) > 7:
                continue
            g = na_rpb[:, :, dy + 7, :][:, :, dx]
            out[:, :, kl, :, sp, :] = np.where(col_in[None, None], g, np.float32(NEG))
    return np.ascontiguousarray(out.reshape(L, 16, 128, 22 * 64))


def make_in_maps(inputs, ncores=4):
    cmask, bmask, a2 = _const_inputs()
    f = lambda a: np.ascontiguousarray(np.asarray(a, dtype=np.float32))
    shared = {
        "w_ada": f(inputs["w_ada"]),
        "b_ada": f(inputs["b_ada"]).reshape(DEPTH, 96, 128),
        "norms": np.ascontiguousarray(np.concatenate([f(inputs["norm_mix"]).reshape(DEPTH, 16, 128),
                                                      f(inputs["norm_ffn"]).reshape(DEPTH, 16, 128)], axis=1)),
        "norm_final": np.ascontiguousarray(np.tile(f(inputs["norm_final"]).reshape(16, 128), (2, 1))),
        "w_in": f(inputs["w_in"]),
        "rpbT": _rpb_tables(f(inputs["na_rpb"])),
        "conv_w": np.ascontiguousarray(np.concatenate([f(inputs["conv_w"]).reshape(DEPTH, 240, 128),
                                                       np.zeros((DEPTH, 16, 128), np.float32)], axis=1)),
        "conv_b": np.ascontiguousarray(np.concatenate([f(inputs["conv_b"]).reshape(DEPTH, 48, 128),
                                                       np.zeros((DEPTH, 16, 128), np.float32)], axis=1)),
        "dt_bias": f(inputs["dt_bias"]).reshape(DEPTH, 128),
        "a_log": f(inputs["a_log"]).reshape(DEPTH, 128),
        "d_skip": f(inputs["d_skip"]),
        "ssd_norm": f(inputs["ssd_norm"]),
        "w_br_na": f(inputs["w_br_na"]),
        "w_br_ssd": f(inputs["w_br_ssd"]),
        "w_out": f(inputs["w_out"]),
        "w_gate_up": f(inputs["w_gate_up"]),
        "w_down": f(inputs["w_down"]),
        "cmask": cmask, "bmask": bmask, "a2": a2,
    }
    x = f(inputs["x"])
    ctx = f(inputs["ctx"])
    c = f(inputs["c"])
    cc = f(inputs["c_ctx"])
    maps = []
    for b in range(ncores):
        m = dict(shared)
        m["x_tok"] = np.ascontiguousarray(np.concatenate([ctx[b], x[b]], axis=0))
        m["c2"] = np.ascontiguousarray(np.stack([c[b], cc]).reshape(32, 128))
        maps.append(m)
    return maps


def kernel(**inputs):
    nc = Builder().build()
    maps = make_in_maps(inputs, 4)
    res = run_bass_kernel_spmd(nc, maps, core_ids=list(range(4)))
    return np.stack([np.asarray(r["out"], dtype=np.float32) for r in res.results], axis=0)
```
